# Optimizing a Trainium2 kernel written in Bass

```python
import jax, jax.numpy as jnp
from jax import lax
import numpy as np

D_MODEL = 1024
BATCH = 8
SEQ = 4096
DEPTH = 2

CHUNK = 64
P_DIM = 256
EPS = 1e-6
HG_HEADS = 6
HG_DK = 128
HG_DV = 128
D_HG = HG_HEADS * HG_DV
POOL_WINDOWS = (2, 4, 8, 16)
POOL_GROUPS = 4
POOL_CH = 128
D_POOL = POOL_GROUPS * POOL_CH
SSD_HEADS = 12
SSD_HEAD_DIM = 64
D_SSD = SSD_HEADS * SSD_HEAD_DIM
SSD_STATE = 128
SSD_GROUPS = 4
SSD_HPG = SSD_HEADS // SSD_GROUPS
SSD_CONV = 4
SSD_CONV_DIM = D_SSD + 2 * SSD_GROUPS * SSD_STATE
D_MIX = D_HG + D_POOL + D_SSD
IN_SPLITS = (HG_HEADS * HG_DK, HG_HEADS * HG_DK, D_HG, D_HG, D_POOL, D_POOL, SSD_CONV_DIM, SSD_HEADS, D_SSD)
N_IN = 6668

kernel_name = "hymba_style_hgrn2_pool_ssd_trunk"


def rms_norm(x, w):
    x32 = x.astype(jnp.float32)
    y = x32 * lax.rsqrt(jnp.mean(x32 * x32, axis=-1, keepdims=True) + EPS)
    return (y * w.astype(jnp.float32)).astype(x.dtype)


def group_rms_norm(y, w, groups):
    b, s, d = y.shape
    y32 = y.astype(jnp.float32).reshape(b, s, groups, d // groups)
    y32 = y32 * lax.rsqrt(jnp.mean(y32 * y32, axis=-1, keepdims=True) + EPS)
    return y32.reshape(b, s, d) * w.astype(jnp.float32)


def hgrn2_mixer(q, f_logit, i_in, lb):
    b, s, _ = q.shape
    nc = s // CHUNK
    lb = lb.astype(jnp.float32)
    log_f = jnp.logaddexp(jnp.log(lb), jnp.log1p(-lb) + jax.nn.log_sigmoid(f_logit.astype(jnp.float32)))
    k = -jnp.expm1(log_f)

    def to_chunks(t, d):
        return t.astype(jnp.float32).reshape(b, nc, CHUNK, HG_HEADS, d).transpose(1, 0, 3, 2, 4)

    qc, kc, vc, gc = to_chunks(q, HG_DK), to_chunks(k, HG_DK), to_chunks(i_in, HG_DV), to_chunks(log_f, HG_DK)
    causal = jnp.tril(jnp.ones((CHUNK, CHUNK), dtype=bool))

    def step(state, inp):
        qt, kt, vt, gt = inp
        cum = jnp.cumsum(gt, axis=2)
        inter = jnp.einsum('bhtk,bhkv->bhtv', qt * jnp.exp(cum), state)
        rel = cum[:, :, :, None, :] - cum[:, :, None, :, :]
        decay = jnp.exp(jnp.where(causal[:, :, None], rel, -jnp.inf))
        scores = jnp.einsum('bhtk,bhsk,bhtsk->bhts', qt, kt, decay)
        intra = jnp.einsum('bhts,bhsv->bhtv', scores, vt)
        last = cum[:, :, -1:, :]
        state = jnp.exp(last[:, :, 0, :])[..., None] * state + jnp.einsum(
            'bhsk,bhsv->bhkv', kt * jnp.exp(last - cum), vt)
        return state, inter + intra

    s0 = jnp.zeros((b, HG_HEADS, HG_DK, HG_DV), jnp.float32)
    _, o = lax.scan(step, s0, (qc, kc, vc, gc))
    return o.transpose(1, 0, 3, 2, 4).reshape(b, s, D_HG)


def pool_mixer(u, pool_w, pool_scale):
    b, s, _ = u.shape
    u32 = u.astype(jnp.float32).reshape(b, s, POOL_GROUPS, POOL_CH)
    cs = jnp.cumsum(u32, axis=1)
    pos = jnp.arange(1, s + 1, dtype=jnp.float32)
    outs = []
    for g, win in enumerate(POOL_WINDOWS):
        c = cs[:, :, g, :]
        lagged = jnp.pad(c, ((0, 0), (win, 0), (0, 0)))[:, :s]
        count = jnp.minimum(pos, float(win))[None, :, None]
        outs.append((c - lagged) / count - u32[:, :, g, :])
    pooled = jnp.stack(outs, axis=2)
    y = jnp.einsum('bsgc,gcd->bsgd', pooled, pool_w.astype(jnp.float32)) * pool_scale.astype(jnp.float32)
    return y.reshape(b, s, D_POOL)


def ssd_mixer(xbc, dt_raw, conv_w, conv_b, dt_bias, a_log, d_skip):
    b, s, _ = xbc.shape
    nc = s // CHUNK
    xbc = lax.conv_general_dilated(
        xbc, conv_w[:, None, :], window_strides=(1,), padding=[(SSD_CONV - 1, 0)],
        dimension_numbers=('NWC', 'WIO', 'NWC'), feature_group_count=SSD_CONV_DIM) + conv_b
    xbc = jax.nn.silu(xbc.astype(jnp.float32))
    xs, bm, cm = jnp.split(xbc, [D_SSD, D_SSD + SSD_GROUPS * SSD_STATE], axis=-1)
    xs = xs.reshape(b, nc, CHUNK, SSD_GROUPS, SSD_HPG, SSD_HEAD_DIM)
    bm = bm.reshape(b, nc, CHUNK, SSD_GROUPS, SSD_STATE)
    cm = cm.reshape(b, nc, CHUNK, SSD_GROUPS, SSD_STATE)
    dt = jax.nn.softplus(dt_raw.astype(jnp.float32) + dt_bias.astype(jnp.float32))
    a = -jnp.exp(a_log.astype(jnp.float32))
    dt_c = dt.reshape(b, nc, CHUNK, SSD_GROUPS, SSD_HPG)
    dta = (dt_c * a.reshape(SSD_GROUPS, SSD_HPG)).transpose(0, 3, 4, 1, 2)
    xdt = xs * dt_c[..., None]
    cum = jnp.cumsum(dta, axis=-1)
    causal = jnp.tril(jnp.ones((CHUNK, CHUNK), dtype=bool))
    seg = jnp.exp(jnp.where(causal, cum[..., :, None] - cum[..., None, :], -jnp.inf))
    cb = jnp.einsum('bctgn,bcsgn->bgcts', cm, bm)
    y_diag = jnp.einsum('bgcts,bgrcts,bcsgrp->bctgrp', cb, seg, xdt)
    last = cum[..., -1:]
    states = jnp.einsum('bcsgn,bgrcs,bcsgrp->bcgrpn', bm, jnp.exp(last - cum), xdt)
    chunk_decay = jnp.exp(last[..., 0])

    def carry(h, inp):
        dec, st = inp
        return dec[..., None, None] * h + st, h

    h0 = jnp.zeros((b, SSD_GROUPS, SSD_HPG, SSD_HEAD_DIM, SSD_STATE), jnp.float32)
    _, h_prev = lax.scan(carry, h0, (chunk_decay.transpose(3, 0, 1, 2), states.transpose(1, 0, 2, 3, 4, 5)))
    y_off = jnp.einsum('bctgn,bgrct,cbgrpn->bctgrp', cm, jnp.exp(cum), h_prev)
    y = y_diag + y_off + xs * d_skip.astype(jnp.float32).reshape(SSD_GROUPS, SSD_HPG)[:, :, None]
    return y.reshape(b, s, D_SSD)


def hybrid_layer(h, p_i, lb, norm_w, w_in, hg_norm_w, pool_w, pool_scale, conv_w, conv_b,
                 dt_bias, a_log, d_skip, ssd_norm_w, w_out, w_pe, w_pg):
    u = rms_norm(h, norm_w)
    proj = u @ w_in
    q, f_logit, i_in, g_hg, u_pool, g_pool, xbc, dt_raw, z = jnp.split(
        proj, [int(v) for v in np.cumsum(IN_SPLITS)[:-1]], axis=-1)
    o_hg = group_rms_norm(hgrn2_mixer(q, f_logit, i_in, lb), hg_norm_w, HG_HEADS) * jax.nn.silu(g_hg.astype(jnp.float32))
    o_pool = pool_mixer(u_pool, pool_w, pool_scale) * jax.nn.silu(g_pool.astype(jnp.float32))
    y_ssd = ssd_mixer(xbc, dt_raw, conv_w, conv_b, dt_bias, a_log, d_skip) * jax.nn.silu(z.astype(jnp.float32))
    o_ssd = group_rms_norm(y_ssd, ssd_norm_w, SSD_GROUPS)
    mixed = jnp.concatenate([o_hg, o_pool, o_ssd], axis=-1).astype(h.dtype)
    h = h + mixed @ w_out
    gate = jax.nn.sigmoid(h @ w_pg)
    return h + gate * (p_i @ w_pe)


def setup_inputs(seed: int = 0) -> dict:
    key = jax.random.key(seed)
    ks = jax.random.split(key, 20)
    f32 = jnp.float32
    nrm = lambda k, shape, scale: jax.random.normal(k, shape, f32) * scale
    dt0 = jnp.exp(jax.random.uniform(ks[9], (DEPTH, SSD_HEADS), f32, np.log(1e-3), np.log(1e-1)))
    return {
        'x': nrm(ks[0], (BATCH, SEQ, D_MODEL), 1.0),
        'p': nrm(ks[1], (DEPTH, BATCH, SEQ, P_DIM), 1.0),
        'norm_w': 1.0 + nrm(ks[2], (DEPTH, D_MODEL), 0.02),
        'w_in': nrm(ks[3], (DEPTH, D_MODEL, N_IN), D_MODEL ** -0.5),
        'hg_lb': 1.0 + nrm(ks[4], (DEPTH, HG_HEADS * HG_DK), 0.5),
        'hg_norm_w': 1.0 + nrm(ks[5], (DEPTH, D_HG), 0.02),
        'pool_w': nrm(ks[6], (DEPTH, POOL_GROUPS, POOL_CH, POOL_CH), POOL_CH ** -0.5),
        'pool_scale': 1.0 + nrm(ks[7], (DEPTH, POOL_GROUPS, POOL_CH), 0.1),
        'conv_w': nrm(ks[8], (DEPTH, SSD_CONV, SSD_CONV_DIM), SSD_CONV ** -0.5),
        'conv_b': nrm(ks[10], (DEPTH, SSD_CONV_DIM), 0.01),
        'dt_bias': dt0 + jnp.log(-jnp.expm1(-dt0)),
        'a_log': jnp.log(jax.random.uniform(ks[11], (DEPTH, SSD_HEADS), f32, 1.0, 16.0)),
        'd_skip': 1.0 + nrm(ks[12], (DEPTH, SSD_HEADS), 0.1),
        'ssd_norm_w': 1.0 + nrm(ks[13], (DEPTH, D_SSD), 0.02),
        'w_out': nrm(ks[14], (DEPTH, D_MIX, D_MODEL), D_MIX ** -0.5),
        'w_pe': nrm(ks[15], (DEPTH, P_DIM, D_MODEL), P_DIM ** -0.5),
        'w_pg': nrm(ks[16], (DEPTH, D_MODEL, D_MODEL), D_MODEL ** -0.5),
        'final_norm_w': 1.0 + nrm(ks[17], (D_MODEL,), 0.02),
    }


def reference(x, p, norm_w, w_in, hg_lb, hg_norm_w, pool_w, pool_scale, conv_w, conv_b,
              dt_bias, a_log, d_skip, ssd_norm_w, w_out, w_pe, w_pg, final_norm_w):
    lb_all = jnp.cumsum(jax.nn.softmax(hg_lb.astype(jnp.float32), axis=0), axis=0)
    lb_all = lb_all - lb_all[0]
    h = x
    for i in range(DEPTH):
        h = hybrid_layer(h, p[i], lb_all[i], norm_w[i], w_in[i], hg_norm_w[i], pool_w[i], pool_scale[i],
                         conv_w[i], conv_b[i], dt_bias[i], a_log[i], d_skip[i], ssd_norm_w[i],
                         w_out[i], w_pe[i], w_pg[i])
    return rms_norm(h, final_norm_w)
```

```python
import numpy as np
import concourse.bass as bass
import concourse.mybir as mybir
from concourse.bass_utils import run_bass_kernel_spmd

F32 = mybir.dt.float32
BF16 = mybir.dt.bfloat16
AF = mybir.ActivationFunctionType
ALU = mybir.AluOpType

S_LEN = 4096
D = 1024
T = 512
NB = T // 128
NCH = T // 64
NT = S_LEN // T
EPS = 1e-6
POOL_W = (2, 4, 8, 16)
FW = 528

_o = np.cumsum([0, 768, 768, 768, 768, 512, 512, 1792, 12, 768])
O_Q, O_F, O_I, O_G, O_UP, O_GP, O_XBC, O_DT, O_Z = [int(v) for v in _o[:9]]


def _rng(a, n):
    return list(range(a, a + n))


def _in_groups():
    g = []
    g.append(("V0", _rng(O_I, 384)))
    g.append(("V1", _rng(O_I + 384, 384)))
    g.append(("DT", _rng(O_DT, 12) + [-1] * (384 - 12)))
    for hp in range(3):
        cols = []
        for h in (2 * hp, 2 * hp + 1):
            cols += _rng(O_F + 128 * h, 128) + _rng(O_Q + 128 * h, 128)
        g.append((f"FQ{hp}", cols))
    g.append(("G0", _rng(O_G, 384)))
    g.append(("P0", _rng(O_UP, 256) + _rng(O_GP, 256)))
    g.append(("G1", _rng(O_G + 384, 384)))
    g.append(("P1", _rng(O_UP + 256, 256) + _rng(O_GP + 256, 256)))
    g.append(("SX0", _rng(O_XBC, 512)))
    g.append(("SX1", _rng(O_XBC + 512, 256) + [-1] * 128))
    g.append(("SB", _rng(O_XBC + 768, 512)))
    g.append(("SC", _rng(O_XBC + 1280, 512)))
    g.append(("Z0", _rng(O_Z, 384)))
    g.append(("Z1", _rng(O_Z + 384, 384)))
    return g


IN_GROUPS = _in_groups()
GROUP_ORDER = [n for n, _ in IN_GROUPS] + ["WO0", "WO1", "WO2", "WO3", "WG0", "WPE", "WG1"]
GROUP_SIZE = {n: 8 * len(c) for n, c in IN_GROUPS}
for _n in ("WO0", "WO1", "WO2", "WO3", "WG0", "WG1"):
    GROUP_SIZE[_n] = 4096
GROUP_SIZE["WPE"] = 2048
GROUP_OFF = {}
_acc = 0
for _n in GROUP_ORDER:
    GROUP_OFF[_n] = _acc
    _acc += GROUP_SIZE[_n]
NW = _acc

NV = 150
V_NW, V_HGN, V_LBA, V_LBB, V_PSC, V_CW, V_CB, V_SSDN, V_FNW, V_DTB, V_ALOG, V_DSK = 0, 8, 14, 20, 26, 30, 86, 100, 106, 114, 126, 138


def _prep_weights(w_in, w_out, w_pe, w_pg):
    L = w_in.shape[0]
    out = np.zeros((L, 128, NW), np.float32)
    for l in range(L):
        wi = w_in[l].reshape(8, 128, -1)
        for name, cols in IN_GROUPS:
            n = len(cols)
            blk = np.zeros((128, 8, n), np.float32)
            idx = np.array([c for c in cols if c >= 0])
            blk[:, :, : len(idx)] = wi[:, :, idx].transpose(1, 0, 2)
            out[l, :, GROUP_OFF[name]: GROUP_OFF[name] + 8 * n] = blk.reshape(128, -1)
        wo = w_out[l].reshape(16, 128, 1024)
        for g in range(4):
            blk = wo[:, :, g * 256:(g + 1) * 256].transpose(1, 0, 2)
            out[l, :, GROUP_OFF[f"WO{g}"]: GROUP_OFF[f"WO{g}"] + 4096] = blk.reshape(128, -1)
        wg = w_pg[l].reshape(8, 128, 1024)
        for g in range(2):
            blk = wg[:, :, g * 512:(g + 1) * 512].transpose(1, 0, 2)
            out[l, :, GROUP_OFF[f"WG{g}"]: GROUP_OFF[f"WG{g}"] + 4096] = blk.reshape(128, -1)
        wp = w_pe[l].reshape(2, 128, 1024).transpose(1, 0, 2)
        out[l, :, GROUP_OFF["WPE"]: GROUP_OFF["WPE"] + 2048] = wp.reshape(128, -1)
    return out


def _prep_vecs(norm_w, hg_lb, hg_norm_w, pool_scale, conv_w, conv_b, dt_bias, a_log, d_skip, ssd_norm_w, final_norm_w):
    L = norm_w.shape[0]
    v = np.zeros((L, 128, NV), np.float32)
    for l in range(L):
        v[l, :, V_NW:V_NW + 8] = norm_w[l].reshape(8, 128).T
        v[l, :, V_HGN:V_HGN + 6] = hg_norm_w[l].reshape(6, 128).T
        v[l, :, V_LBA:V_LBA + 6] = hg_lb[0].reshape(6, 128).T
        v[l, :, V_LBB:V_LBB + 6] = hg_lb[l].reshape(6, 128).T
        v[l, :, V_PSC:V_PSC + 4] = pool_scale[l].reshape(4, 128).T
        cw = conv_w[l].reshape(4, 14, 128)
        v[l, :, V_CW:V_CW + 56] = cw.transpose(2, 0, 1).reshape(128, 56)
        v[l, :, V_CB:V_CB + 14] = conv_b[l].reshape(14, 128).T
        v[l, :, V_SSDN:V_SSDN + 6] = ssd_norm_w[l].reshape(6, 128).T
        v[l, :, V_FNW:V_FNW + 8] = final_norm_w.reshape(8, 128).T
        v[l, :, V_DTB:V_DTB + 12] = dt_bias[l][None, :]
        v[l, :, V_ALOG:V_ALOG + 12] = a_log[l][None, :]
        v[l, :, V_DSK:V_DSK + 12] = d_skip[l][None, :]
    return v


class _Op:
    __slots__ = ("eng", "fn", "deps", "dma_slot", "idx", "sig", "sigval", "waits", "multi", "raw", "epos")


class Prog:
    ENGS = ("pe", "act", "dve", "pool", "sp")
    DEPTH = {"pe": 0, "act": 16, "dve": 16, "pool": 16, "sp": 0}

    def __init__(self, nc):
        self.nc = nc
        self.ops = []
        self.state = {}
        self.dma_count = {}

    def add(self, eng, fn, R=(), W=(), dma_slot=None, multi=False):
        op = _Op()
        op.eng, op.fn, op.dma_slot, op.multi = eng, fn, dma_slot, multi
        op.idx = len(self.ops)
        op.sig = False
        op.sigval = None
        deps = {}
        raw = set()
        for k in R:
            st = self.state.get(k)
            if st is not None and st[0] is not None:
                deps[st[0].idx] = st[0]
                raw.add(st[0].idx)
        op.raw = raw
        for k in W:
            st = self.state.get(k)
            if st is not None:
                if st[0] is not None:
                    deps[st[0].idx] = st[0]
                for r in st[1]:
                    deps[r.idx] = r
        deps.pop(op.idx, None)
        op.deps = list(deps.values())
        for k in R:
            st = self.state.setdefault(k, [None, []])
            st[1].append(op)
        for k in W:
            self.state[k] = [op, []]
        if dma_slot is not None:
            c = self.dma_count.get(dma_slot, 0) + 1
            self.dma_count[dma_slot] = c
            op.sig = True
            op.sigval = 16 * c
        self.ops.append(op)
        return op

    def emit(self, final_waits=()):
        nc = self.nc
        epos = {e: 0 for e in self.ENGS}
        for op in self.ops:
            op.epos = epos[op.eng]
            epos[op.eng] += 1

        def same_eng_needs(op, d):
            return (op.eng != "pe" and d.dma_slot is None and op.dma_slot is None
                    and op.epos - d.epos <= self.DEPTH[op.eng])

        for op in self.ops:
            for d in op.deps:
                if d.dma_slot is None and (d.eng != op.eng or op.dma_slot is not None or same_eng_needs(op, d)):
                    d.sig = True
        cnt = {e: 0 for e in self.ENGS}
        for op in self.ops:
            if op.dma_slot is None and op.sig:
                cnt[op.eng] += 1
                op.sigval = cnt[op.eng]
        known = {e: {} for e in self.ENGS}
        snap = {}
        for op in self.ops:
            kn = known[op.eng]
            waits = []
            for d in op.deps:
                if d.dma_slot is None:
                    if d.eng == op.eng and op.dma_slot is None and not same_eng_needs(op, d):
                        continue
                    key = "E_" + d.eng
                else:
                    key = "D_" + d.dma_slot
                if kn.get(key, 0) >= d.sigval:
                    continue
                waits.append((key, d.sigval))
                kn[key] = d.sigval
                sn = snap.get(d.idx)
                if sn is not None:
                    for k2, v2 in sn.items():
                        if kn.get(k2, 0) < v2:
                            kn[k2] = v2
            wm = {}
            for k, v in waits:
                wm[k] = max(wm.get(k, 0), v)
            op.waits = list(wm.items())
            if op.sig and op.dma_slot is None:
                kn2 = dict(kn)
                snap[op.idx] = kn2
        sem_names = ["E_" + e for e in self.ENGS] + ["D_" + s for s in self.dma_count]
        from contextlib import ExitStack
        with ExitStack() as es:
            sems = {}
            for n in sem_names:
                sems[n] = es.enter_context(nc.semaphore(n))
            block = es.enter_context(nc.Block())
            byeng = {e: [o for o in self.ops if o.eng == e] for e in self.ENGS}

            def run(ename, eh):
                for op in byeng[ename]:
                    ws = op.waits
                    if op.multi or len(ws) > 1 or ename in ("pe", "sp") or op.dma_slot is not None:
                        nattach = 0 if (op.multi or ename in ("pe", "sp") or op.dma_slot is not None) else 1
                        for k, v in ws[nattach:]:
                            eh.wait_ge(sems[k], v)
                        ws = ws[:nattach]
                    ins = op.fn(eh)
                    for k, v in ws:
                        ins._wait_ge(sems[k], v)
                    if op.sig:
                        if op.dma_slot is not None:
                            ins.then_inc(sems["D_" + op.dma_slot], 16)
                        else:
                            ins.then_inc(sems["E_" + ename], 1)
                if ename == "sp":
                    for slot in final_waits:
                        eh.wait_ge(sems["D_" + slot], 16 * self.dma_count[slot])

            @block.tensor
            def _(e):
                run("pe", e)

            @block.scalar
            def _(e):
                run("act", e)

            @block.vector
            def _(e):
                run("dve", e)

            @block.gpsimd
            def _(e):
                run("pool", e)

            @block.sync
            def _(e):
                run("sp", e)


class _Stop(Exception):
    pass


def _interleave_gen(tasks, width):
    tasks = iter(tasks)
    live = []
    done = False
    while True:
        while not done and len(live) < width:
            try:
                live.append(next(tasks))
            except StopIteration:
                done = True
        if not live:
            break
        for g in list(live):
            try:
                next(g)
            except StopIteration:
                live.remove(g)
            yield


def _interleave_w(gens_weights):
    live = [[g, w] for g, w in gens_weights]
    while live:
        for item in list(live):
            for _ in range(item[1]):
                try:
                    next(item[0])
                except StopIteration:
                    live.remove(item)
                    break


def _interleave(tasks, width):
    tasks = iter(tasks)
    live = []
    done = False
    while True:
        while not done and len(live) < width:
            try:
                live.append(next(tasks))
            except StopIteration:
                done = True
        if not live:
            break
        for g in list(live):
            try:
                next(g)
            except StopIteration:
                live.remove(g)


def build_program(n_tiles=NT, n_layers=2, dbg=None, stop=None):
    nc = bass.Bass("TRN2", target_bir_lowering=False)
    pr = Prog(nc)
    x_d = nc.dram_tensor("x", [S_LEN, D], F32, kind="ExternalInput").ap()
    p_d = nc.dram_tensor("p", [2, S_LEN, 256], F32, kind="ExternalInput").ap()
    wcat_d = nc.dram_tensor("wcat", [2, 128, NW], F32, kind="ExternalInput").ap()
    vecs_d = nc.dram_tensor("vecs", [2, 128, NV], F32, kind="ExternalInput").ap()
    poolw_d = nc.dram_tensor("poolw", [2, 128, 512], F32, kind="ExternalInput").ap()
    out_d = nc.dram_tensor("out", [S_LEN, D], F32, kind="ExternalOutput").ap()
    wbf_d = nc.dram_tensor("wbf", [2, 128, NW], BF16, kind="Internal").ap()
    dbg_d = None
    if dbg is not None:
        dbg_d = nc.dram_tensor("dbg", [128, 8, T], F32, kind="ExternalOutput").ap()

    def sb(name, shape, dt):
        return nc.alloc_sbuf_tensor(name, shape, dt).ap()

    hT = sb("hT", [128, 8, T], F32)
    uT = sb("uT", [128, 8, T], BF16)
    mixraw = sb("mixraw", [128, 16 * T], BF16)
    mixed = mixraw.rearrange("p (k t) -> p k t", t=T)
    iof = mixraw.bitcast(F32).rearrange("p (j d) -> p j d", d=D)
    MIXK = [("mixed", k) for k in range(16)]
    pin = sb("pin", [128, NB, 256], F32)
    pT = sb("pT", [128, 2, T], BF16)
    wsl = [sb(f"wsl{i}", [128, 4096], BF16) for i in range(4)]
    Sst = sb("Sst", [128, 2, 6, 128], F32)
    hst = sb("hst", [128, 2, 4, 192], F32)
    hbf = sb("hbf", [128, 2, 4, 192], BF16)
    phalo = sb("phalo", [128, 2, 4, 16], F32)
    chalo = sb("chalo", [128, 2, 14, 4], F32)
    vec = sb("vec", [128, 2, NV], F32)
    lbv = sb("lbv", [128, 2, 3, 6], F32)
    a_b = sb("a_b", [128, 2, 12], F32)
    idsk = sb("idsk", [128, 2, 12, 128], BF16)
    pwb = sb("pwb", [128, 2, 4, 128], BF16)
    cfx = sb("cfx", [128, 4, 16], F32)
    identF = sb("identF", [128, 128], F32)
    identB = sb("identB", [128, 128], BF16)
    onesB = sb("onesB", [128, 128], BF16)
    mask01 = sb("mask01", [128, 128], BF16)
    mneg = sb("mneg", [128, 128], BF16)
    tri = sb("tri", [128, 128], F32)
    ind = sb("ind", [128, 2, 128], F32)
    smask = sb("smask", [128, T], F32)
    qT = sb("qT", [128, 6, T], BF16)
    kT = sb("kT", [128, 6, T], BF16)
    vtm = sb("vtm", [128, NB, 768], BF16)
    xs = sb("xs", [128, 6, T], BF16)
    bc = sb("bc", [128, 8, T], BF16)
    xstm = sb("xstm", [128, NB, 768], BF16)
    D4 = sb("D4", [128, 3072], BF16)
    sg = D4.rearrange("p (h t) -> p h t", t=T)
    xdtd = D4.rearrange("p (j n) -> p j n", n=768)
    E4 = sb("E4", [128, 2048], BF16)
    ktm = E4[:, 0:1024].rearrange("p (a j k) -> p a j k", a=2, j=4)
    bmtm = E4.rearrange("p (j n) -> p j n", n=512)
    Ft = sb("Ft", [128, 6, FW], F32)
    G4 = sb("G4", [128, NB, 768], BF16)
    pb = sb("pb", [128, 4, T], BF16)
    AT = sb("AT", [128, 2, 4, 128], BF16)
    Sp = sb("Sp", [128, 2, 8, 128], BF16)
    smH = sb("smH", [128, 6, 3, 8], F32)
    seg = sb("seg", [128, 8, 128], F32)
    LTb = sb("LTb", [128, 8, 128], BF16)
    ysb = sb("ysb", [128, 2, 768], F32)
    otm = sb("otm", [128, 2, 768], BF16)
    tmS = sb("tmS", [128, 9, 48], F32)
    dcyb = sb("dcyb", [128, 2, 48], F32)
    ssg = sb("ssg", [128, 2, 8], F32)
    dhl = sb("dhl", [128, 2, 48], BF16)
    ytmp = sb("ytmp", [128, 2, 192], F32)

    pp = [nc.alloc_psum_tensor(f"pp{i}", [128, 512], F32).ap() for i in range(3)]
    mb = [nc.alloc_psum_tensor(f"mb{i}", [128, 512], F32).ap() for i in range(5)]
    mbbf = [m.bitcast(BF16) for m in mb]
    ppi = [0]
    MB0 = [("mb0", i) for i in range(4)]
    MB2 = [("mb2", i) for i in range(4)]

    ppmod = [3]

    def next_pp():
        i = ppi[0] % ppmod[0]
        ppi[0] += 1
        return pp[i], f"pp{i}"

    def act(out, in_, func, R, W, bias=0.0, scale=1.0, accum_out=None):
        if accum_out is None:
            pr.add("act", lambda e: e.activation(out=out, in_=in_, func=func, bias=bias, scale=scale), R, W)
        else:
            pr.add("act", lambda e: e.activation(out=out, in_=in_, func=func, bias=bias, scale=scale,
                                                 accum_out=accum_out), R, W, multi=True)

    def tt(eng, out, in0, in1, op, R, W):
        pr.add(eng, lambda e: e.tensor_tensor(out=out, in0=in0, in1=in1, op=op), R, W)

    def ts(eng, out, in0, s1, s2, op0, op1, R, W):
        if op1 is None:
            pr.add(eng, lambda e: e.tensor_scalar(out=out, in0=in0, scalar1=s1, scalar2=None, op0=op0), R, W)
        else:
            pr.add(eng, lambda e: e.tensor_scalar(out=out, in0=in0, scalar1=s1, scalar2=s2, op0=op0, op1=op1), R, W)

    def stt(out, in0, scalar, in1, op0, op1, R, W):
        pr.add("dve", lambda e: e.scalar_tensor_tensor(out=out, in0=in0, scalar=scalar, in1=in1, op0=op0, op1=op1), R, W)

    def cp(eng, out, in_, R, W):
        if eng == "act":
            pr.add("act", lambda e: e.activation(out=out, in_=in_, func=AF.Copy), R, W)
        else:
            pr.add(eng, lambda e: e.tensor_copy(out=out, in_=in_), R, W)

    def mm(out, lhsT, rhs, start, stop, R, W):
        pr.add("pe", lambda e: e.matmul(out, lhsT=lhsT, rhs=rhs, start=start, stop=stop), R, W)

    def tp(out, in_, ident, R, W):
        pr.add("pe", lambda e: e.transpose(out, in_, ident), R, W)

    def dma(q, out, in_, R, W, slot):
        pr.add(q, lambda e: e.dma_start(out=out, in_=in_), R, W, dma_slot=slot)

    def memset(eng, ap, val, W):
        pr.add(eng, lambda e: e.memset(ap, val), (), W)

    for L in range(2):
        dma("sp", vec[:, L, :], vecs_d[L], (), ("vec",), "vec")
    ncv = 0
    for L in range(n_layers):
        for gname in GROUP_ORDER:
            off, sz = GROUP_OFF[gname], GROUP_SIZE[gname]
            src = wcat_d[L][:, off:off + sz].rearrange("p (k n) -> p k n", n=1024)
            dst = wbf_d[L][:, off:off + sz].rearrange("p (k n) -> p k n", n=1024)
            rk = [("cvtok", ncv - 4)] if ncv >= 4 else []
            dma("pool", dst, src, rk, [("wbf", L, gname), ("cvtok", ncv)], f"cv{L}{gname}")
            ncv += 1
    for L in range(2):
        dma("pool", pwb[:, L, :, :].rearrange("p g d -> p (g d)"), poolw_d[L], (), ("pwb",), "pwb")
    memset("pool", Sst, 0.0, [("Sst", L, h) for L in range(2) for h in range(6)])
    memset("pool", hst, 0.0, [("hst", L, g) for L in range(2) for g in range(4)])
    memset("pool", hbf, 0.0, [("hbf", L, g) for L in range(2) for g in range(4)])
    memset("pool", phalo, 0.0, ["phalo"])
    memset("pool", chalo, 0.0, ["chalo"])
    memset("pool", onesB, 1.0, ["const"])
    memset("pool", identF, 1.0, ["const"])
    pr.add("pool", lambda e: e.affine_select(out=identF, in_=identF, pattern=[[1, 128]], compare_op=ALU.is_ge,
                                             fill=0.0, base=0, channel_multiplier=-1), ["const"], ["const"])
    cp("pool", tri, identF, ["const"], ["const"])
    pr.add("pool", lambda e: e.affine_select(out=identF, in_=identF, pattern=[[-1, 128]], compare_op=ALU.is_ge,
                                             fill=0.0, base=0, channel_multiplier=1), ["const"], ["const"])
    cp("pool", identB, identF, ["const"], ["const"])
    memset("pool", tri[0:64, 64:128], 0.0, ["const"])
    cp("pool", mask01, tri, ["const"], ["const"])
    ts("pool", mneg, tri, -1.0, 1.0e5, ALU.add, ALU.mult, ["const"], ["const"])
    memset("pool", ind, 0.0, ["const"])
    memset("pool", ind[0:64, 0, :], 1.0, ["const"])
    memset("pool", ind[64:128, 1, :], 1.0, ["const"])
    memset("pool", smask, 1.0, ["const"])
    memset("pool", smask.rearrange("p (c t) -> p c t", t=64)[:, :, 0:1], 0.0, ["const"])
    memset("pool", cfx, 1.0, ["const"])
    for g, w in enumerate(POOL_W):
        for t_ in range(w - 1):
            memset("pool", cfx[:, g, t_:t_ + 1], float(w) / float(t_ + 1), ["const"])
    memset("dve", lbv[:, 0, 0, :], 0.0, ["lbv"])
    memset("dve", lbv[:, 0, 1, :], 1.0, ["lbv"])
    memset("dve", lbv[:, 0, 2, :], -1.0, ["lbv"])
    tt("dve", lbv[:, 1, 0, :], vec[:, 1, V_LBA:V_LBA + 6], vec[:, 1, V_LBB:V_LBB + 6], ALU.subtract, ["vec"], ["lbv"])
    act(lbv[:, 1, 0, :], lbv[:, 1, 0, :], AF.Exp, ["lbv"], ["lbv"])
    ts("dve", lbv[:, 1, 0, :], lbv[:, 1, 0, :], 1.0, None, ALU.add, None, ["lbv"], ["lbv"])
    pr.add("dve", lambda e: e.reciprocal(out=lbv[:, 1, 0, :], in_=lbv[:, 1, 0, :]), ["lbv"], ["lbv"])
    ts("dve", lbv[:, 1, 1, :], lbv[:, 1, 0, :], -1.0, 1.0, ALU.mult, ALU.add, ["lbv"], ["lbv"])
    ts("dve", lbv[:, 1, 2, :], lbv[:, 1, 1, :], -1.0, None, ALU.mult, None, ["lbv"], ["lbv"])
    for L in range(2):
        act(a_b[:, L, :], vec[:, L, V_ALOG:V_ALOG + 12], AF.Exp, ["vec"], ["a_b"])
        ts("dve", a_b[:, L, :], a_b[:, L, :], -1.0, None, ALU.mult, None, ["a_b"], ["a_b"])
        for h in range(12):
            ts("dve", idsk[:, L, h, :], identF, vec[:, L, V_DSK + h:V_DSK + h + 1], None, ALU.mult, None,
               ["const", "vec"], ["idsk"])

    wseq = []
    for it in range(n_tiles):
        for L in range(n_layers):
            for gname in GROUP_ORDER:
                wseq.append((L, gname))
    wstate = {"issued": 0, "used": 0}

    def use_w(expect):
        k = wstate["used"]
        while wstate["issued"] < min(len(wseq), k + 3):
            j = wstate["issued"]
            L_, g_ = wseq[j]
            off, sz = GROUP_OFF[g_], GROUP_SIZE[g_]
            s = j % 4
            dma("sp", wsl[s][:, 0:sz], wbf_d[L_][:, off:off + sz], [("wbf", L_, g_)], [("wsl", s)], f"w{s}")
            wstate["issued"] += 1
        wstate["used"] += 1
        assert wseq[k][1] == expect, (wseq[k], expect)
        return wsl[k % 4], ("wsl", k % 4)

    def wview(slot, n):
        return slot[:, 0:8 * n].rearrange("p (c n) -> p c n", c=8)

    def tile_layer(it, L, last_layer):
        t0 = it * T

        def V(a, n=1):
            return vec[:, L, a:a + n]

        def chk(name):
            if stop == name and it == 0 and L == 0:
                dma("sp", dbg_d, hT, [("hT", c) for c in range(8)], ["dbgd"], "dbg")
                raise _Stop()

        def rms_stats(nchunk, scale):
            bank, bk = next_pp()
            for c in range(nchunk):
                act(uT[:, c, :], hT[:, c, :], AF.Square, [("hT", c)], [("uT", c)])
            for c in range(nchunk):
                mm(bank, onesB, uT[:, c, :], c == 0, c == nchunk - 1, [("uT", c), "const"], [bk])
            act(Ft[:, 5, 0:T], bank, AF.Ln, [bk], [("Ft", 5)], bias=EPS, scale=scale)
            act(Ft[:, 5, 0:T], Ft[:, 5, 0:T], AF.Exp, [("Ft", 5)], [("Ft", 5)], scale=-0.5)

        if L == 0:
            dma("sp", iof, x_d[t0:t0 + T, :].rearrange("(j p) d -> p j d", p=128), (), MIXK, "xin")
            for c in range(8):
                bank, bk = next_pp()
                for j in range(NB):
                    tp(bank[:, j * 128:(j + 1) * 128], iof[:, j, c * 128:(c + 1) * 128], identF, MIXK + ["const"], [bk])
                cp("act" if c % 2 == 0 else "dve", hT[:, c, :], bank, [bk], [("hT", c)])
        dma("sp", pin, p_d[L, t0:t0 + T, :].rearrange("(j p) d -> p j d", p=128), (), ["pin"], "pin")
        for jj in range(2):
            bank, bk = next_pp()
            for j in range(NB):
                tp(bank[:, j * 128:(j + 1) * 128], pin[:, j, jj * 128:(jj + 1) * 128], identF, ["pin", "const"], [bk])
            cp("act", pT[:, jj, :], bank, [bk], [("pT", jj)])

        chk("E1")
        rms_stats(8, 1.0 / D)
        for c in range(8):
            stt(uT[:, c, :], hT[:, c, :], V(V_NW + c), Ft[:, 5, 0:T], ALU.mult, ALU.mult,
                [("hT", c), ("Ft", 5), "vec"], [("uT", c)])

        def proj_fm(wv, col0, bank, bk, wk):
            for c in range(8):
                mm(bank, wv[:, c, col0:col0 + 128], uT[:, c, :], c == 0, c == 7, [wk, ("uT", c)], [bk])

        def proj_tm(wv, ncols, j, bank, bk, wk):
            for c in range(8):
                mm(bank[:, 0:ncols], uT[:, c, j * 128:(j + 1) * 128], wv[:, c, 0:ncols], c == 0, c == 7,
                   [wk, ("uT", c)], [bk])

        chk("E2")
        for half in range(2):
            slot, wk = use_w(f"V{half}")
            wv = wview(slot, 384)
            for j in range(NB):
                bank, bk = next_pp()
                proj_tm(wv, 384, j, bank, bk, wk)
                cp("act", vtm[:, j, half * 384:(half + 1) * 384], bank[:, 0:384], [bk], [("vtm", j)])

        chk("E3")
        slot, wk = use_w("DT")
        wv = wview(slot, 384)
        sbk = "mb4"
        for j in range(NB):
            for c in range(8):
                mm(mb[4][:, j * 12:(j + 1) * 12], uT[:, c, j * 128:(j + 1) * 128], wv[:, c, 0:12], c == 0, c == 7,
                   [wk, ("uT", c)], [sbk])
        dt_raw, dt_tm, lndt, dta, cum_tm, bias_tm, ecum, d_tm, dtd = (tmS[:, i, :] for i in range(9))
        TM = ["tmS"]

        def r3(a):
            return a.rearrange("p (j h) -> p j h", h=12)

        def dt_task():
            tt("dve", r3(dt_raw), r3(mb[4][:, 0:48]), V(V_DTB, 12).unsqueeze(1).to_broadcast([128, NB, 12]), ALU.add,
               [sbk, "vec"], TM)
            yield
            act(dt_tm, dt_raw, AF.Exp, TM, TM)
            yield
            act(dt_tm, dt_tm, AF.Ln, TM, TM, bias=1.0)
            yield
            act(lndt, dt_tm, AF.Ln, TM, TM)
            yield
            tt("dve", r3(dta), r3(dt_tm), a_b[:, L, :].unsqueeze(1).to_broadcast([128, NB, 12]), ALU.mult, TM + ["a_b"], TM)
            yield
            cp("dve", dhl[:, 0, :], dta, TM, ["dhl"])
            yield
            tt("dve", dhl[:, 1, :], dta, dhl[:, 0, :], ALU.subtract, TM + ["dhl"], ["dhl"])
            yield
            mm(mb[4][:, 64:112], tri, dta, True, True, TM + ["const"], [sbk])
            yield
            for c in range(2):
                mm(mb[4][:, 128 + 48 * c:176 + 48 * c], ind[:, c, :], dta, True, True, TM + ["const"], [sbk])
            yield
            cp("dve", cum_tm, mb[4][:, 64:112], [sbk], TM)
            yield
            tt("dve", bias_tm, lndt, cum_tm, ALU.subtract, TM, TM)
            yield
            act(ecum, cum_tm, AF.Exp, TM, TM)
            yield
            act(dcyb.rearrange("p c n -> p (c n)"), mb[4][:, 128:224], AF.Exp, [sbk], ["dcyb"])
            yield
            for c in range(2):
                tt("dve", d_tm[c * 64:(c + 1) * 64, :], mb[4][c * 64:(c + 1) * 64, 128 + 48 * c:176 + 48 * c],
                   cum_tm[c * 64:(c + 1) * 64, :], ALU.subtract, [sbk] + TM, TM)
            yield
            act(d_tm, d_tm, AF.Exp, TM, TM)
            yield
            tt("dve", dtd, d_tm, dt_tm, ALU.mult, TM, TM)
            yield

        def fq_task(h, hh, wv, wk):
            fs = 3 * (h % 2)
            t1, t2, t3 = Ft[:, fs, 0:T], Ft[:, fs + 1, 0:T], Ft[:, fs + 2, 0:T]
            k1, k2, k3 = ("Ft", fs), ("Ft", fs + 1), ("Ft", fs + 2)
            smk = ("smH", h)
            lb_, oml_, noml_ = lbv[:, L, 0, h:h + 1], lbv[:, L, 1, h:h + 1], lbv[:, L, 2, h:h + 1]
            bank, bk = next_pp()
            proj_fm(wv, hh * 256, bank, bk, wk)
            act(t1, bank, AF.Exp, [bk], [k1], scale=-1.0)
            yield
            act(t1, t1, AF.Ln, [k1], [k1], bias=1.0)
            yield
            act(t1, t1, AF.Exp, [k1], [k1], scale=-1.0)
            yield
            act(t2, t1, AF.Ln, [k1, "lbv"], [k2], bias=lb_, scale=oml_)
            yield
            ts("dve", t1, t1, noml_, oml_, ALU.mult, ALU.add, [k1, "lbv"], [k1])
            pr.add("dve", lambda e, t3=t3, t2=t2: e.tensor_tensor_scan(out=t3, data0=smask, data1=t2, initial=0.0,
                                                                       op0=ALU.mult, op1=ALU.add),
                   [k2, "const"], [k3])
            yield
            c3 = t3.rearrange("p (c t) -> p c t", t=64)
            act(smH[:, h, 0, :], c3[:, :, 31], AF.Exp, [k3], [smk])
            act(smH[:, h, 1, :], c3[:, :, 63], AF.Exp, [k3], [smk])
            tt("dve", t2.rearrange("p (c t) -> p c t", t=64), c3, c3[:, :, 31:32].to_broadcast([128, NCH, 64]),
               ALU.subtract, [k3], [k2])
            yield
            act(t3, t2, AF.Exp, [k2], [k3])
            act(t2, t2, AF.Exp, [k2], [k2], scale=-1.0)
            cp("act", smH[:, h, 2, :], t3.rearrange("p (c t) -> p c t", t=64)[:, :, 63], [k3], [smk])
            bank2, bk2 = next_pp()
            proj_fm(wv, hh * 256 + 128, bank2, bk2, wk)
            yield
            tt("dve", qT[:, h, :], bank2, t3, ALU.mult, [bk2, k3], [("qT", h)])
            tt("pool", kT[:, h, :], t1, t2, ALU.mult, [k1, k2], [("kT", h)])

        def fq_all():
            for hp in range(3):
                slot, wk = use_w(f"FQ{hp}")
                wv = wview(slot, 512)
                yield from _interleave_gen([fq_task(2 * hp + hh, hh, wv, wk) for hh in range(2)], 2)

        _interleave([fq_all(), dt_task()], 2)

        chk("E4")
        chk("E5")
        def g_all():
            for half in range(2):
                slot, wk = use_w(f"G{half}")
                wv = wview(slot, 384)
                for hh in range(3):
                    h = half * 3 + hh
                    bank, bk = next_pp()
                    proj_fm(wv, hh * 128, bank, bk, wk)
                    yield
                    act(sg[:, h, :], bank, AF.Silu, [bk], [("D4", h)])
                    yield

        def pool_task(g, gg, wv, wk):
            w = POOL_W[g]
            ub, la, lb2 = Ft[:, 3 * gg, :], Ft[:, 3 * gg + 1, :], Ft[:, 3 * gg + 2, :]
            ku, ka, kb = ("Ft", 3 * gg), ("Ft", 3 * gg + 1), ("Ft", 3 * gg + 2)
            NN = 16 + T
            bank, bk = next_pp()
            proj_fm(wv, gg * 128, bank, bk, wk)
            cp("pool", ub[:, 0:16], phalo[:, L, g, :], ["phalo"], [ku])
            yield
            cp("act", ub[:, 16:NN], bank, [bk], [ku])
            yield
            bank2, bk2 = next_pp()
            proj_fm(wv, 256 + gg * 128, bank2, bk2, wk)
            yield
            act(pb[:, gg, :], bank2, AF.Silu, [bk2], [("pb", gg)])
            cp("pool", phalo[:, L, g, :], ub[:, T:NN], [ku], ["phalo"])
            yield
            src_, sk = ub, ku
            dsts = [(la, ka), (lb2, kb)]
            step, lvl = 1, 0
            while step < w:
                dst, dk = dsts[lvl % 2]
                lo = 2 * step - 1
                tt("pool", dst[:, lo:NN], src_[:, lo:NN], src_[:, lo - step:NN - step], ALU.add, [sk], [dk])
                yield
                src_, sk = dst, dk
                step *= 2
                lvl += 1
            if it == 0:
                tt("pool", src_[:, 16:32], src_[:, 16:32], cfx[:, g, :], ALU.mult, [sk, "const"], [sk])
            stt(pb[:, 2 + gg, :], src_[:, 16:NN], 1.0 / w, ub[:, 16:NN], ALU.mult, ALU.subtract,
                [ku, sk], [("pb", 2 + gg)])
            yield
            mm(mb[0], pwb[:, L, g, :], pb[:, 2 + gg, :], True, True, ["pwb", ("pb", 2 + gg)], ["mb0"])
            stt(mixed[:, 6 + g, :], mb[0], V(V_PSC + g), pb[:, gg, :], ALU.mult, ALU.mult,
                ["mb0", ("pb", gg), "vec"], [("mixed", 6 + g)])
            yield

        def pool_all():
            for half in range(2):
                slot, wk = use_w(f"P{half}")
                wv = wview(slot, 512)
                yield from _interleave_gen([pool_task(half * 2 + gg, gg, wv, wk) for gg in range(2)], 2)

        _interleave([g_all(), pool_all()], 2)

        chk("E7")
        rawctr = [0]

        def conv_task(b, bi, wv, wk):
            r = rawctr[0] % 3
            rawctr[0] += 1
            raw, kr = Ft[:, r, :], ("Ft", r)
            bank, bk = next_pp()
            proj_fm(wv, bi * 128, bank, bk, wk)
            cp("pool", raw[:, 0:3], chalo[:, L, b, 0:3], ["chalo"], [kr])
            cp("act", raw[:, 3:3 + T], bank, [bk], [kr])
            act(bank, bank, AF.Identity, [bk, "vec"], [bk], bias=V(V_CB + b), scale=V(V_CW + 3 * 14 + b))
            yield
            for tap in (2, 1, 0):
                stt(bank, raw[:, tap:tap + T], V(V_CW + tap * 14 + b), bank, ALU.mult, ALU.add,
                    [kr, bk, "vec"], [bk])
                yield
            cp("pool", chalo[:, L, b, 0:3], raw[:, T:T + 3], [kr], ["chalo"])
            if b < 6:
                act(xs[:, b, :], bank, AF.Silu, [bk], [("xs", b)])
            else:
                act(bc[:, b - 6, :], bank, AF.Silu, [bk], [("bc", b - 6)])

        def conv_tasks():
            for gname, nblk, blk0, ncols in (("SX0", 4, 0, 512), ("SX1", 2, 4, 384), ("SB", 4, 6, 512), ("SC", 4, 10, 512)):
                slot, wk = use_w(gname)
                wv = wview(slot, ncols)
                for bi in range(nblk):
                    yield conv_task(blk0 + bi, bi, wv, wk)


        chk("E8")
        def z_task(half, j, wv, wk):
            bank, bk = next_pp()
            proj_tm(wv, 384, j, bank, bk, wk)
            yield
            act(G4[:, j, half * 384:(half + 1) * 384], bank[:, 0:384], AF.Silu, [bk], [("G4", j)])

        def z_tasks():
            for half in range(2):
                slot, wk = use_w(f"Z{half}")
                wv = wview(slot, 384)
                for j in range(NB):
                    yield z_task(half, j, wv, wk)

        def stream_b():
            yield from _interleave_gen(conv_tasks(), 2)
            yield from _interleave_gen(z_tasks(), 2)

        chk("E9")
        def hg_stream(heads, KVB):
            obank, obk = pp[2], "pp2"
            for h in heads:
                par = h % 2
                for j in range(NB):
                    js = slice(j * 128, (j + 1) * 128)
                    mm(mb[0][:, js], kT[:, h, js], qT[:, h, js], True, True, [("kT", h), ("qT", h)], ["mb0"])
                tt("dve", AT[:, par, :, :], mb[0].rearrange("p (j t) -> p j t", t=128),
                   mask01.unsqueeze(1).to_broadcast([128, NB, 128]), ALU.mult, ["mb0", "const"], [("AT", par)])
                yield
                for j in range(NB):
                    js = slice(j * 128, (j + 1) * 128)
                    tp(mbbf[0][:, js], kT[:, h, js], identB, [("kT", h), "const"], ["mb0"])
                cp("act", ktm[:, par, :, :].rearrange("p j k -> p (j k)"), mbbf[0][:, 0:512], ["mb0"], [("ktm", par)])
                S = Sst[:, L, h, :]
                SK = ("Sst", L, h)
                act(Sp[:, par, 0, :], S, AF.Identity, [SK, ("smH", h)], [("Sp", par, 0)], scale=smH[:, h, 0, 0:1])
                yield
                for c in range(NCH):
                    j, cc = c // 2, c % 2
                    rows = slice(cc * 64, (cc + 1) * 64)
                    kvb, kvk = KVB[cc]
                    mm(kvb[:, j * 128:(j + 1) * 128], ktm[rows, par, j, :], vtm[rows, j, h * 128:(h + 1) * 128],
                       True, True, [("ktm", par), ("vtm", j)], [kvk])
                yield
                for c in range(NCH):
                    j, cc = c // 2, c % 2
                    kvb, kvk = KVB[cc]
                    kvs = kvb[:, j * 128:(j + 1) * 128]
                    ts("dve", S, S, smH[:, h, 1, c:c + 1], None, ALU.mult, None, [SK, ("smH", h)], [SK])
                    yield
                    stt(S, kvs, smH[:, h, 2, c:c + 1], S, ALU.mult, ALU.add, [kvk, SK, ("smH", h)], [SK])
                    yield
                    if c < NCH - 1:
                        act(Sp[:, par, c + 1, :], S, AF.Identity, [SK, ("smH", h)], [("Sp", par, c + 1)],
                            scale=smH[:, h, 0, c + 1:c + 2])
                for j in range(NB):
                    js = slice(j * 128, (j + 1) * 128)
                    mm(obank[:, js], vtm[:, j, h * 128:(h + 1) * 128], AT[:, par, j, :], j == 0, False,
                       [("vtm", j), ("AT", par)], [obk])
                for c in range(NCH):
                    cs = slice(c * 64, (c + 1) * 64)
                    mm(obank[:, cs], Sp[:, par, c, :], qT[:, h, cs], False, c == NCH - 1,
                       [("Sp", par, c), ("qT", h)], [obk])
                act(pb[:, par, :], obank, AF.Square, [obk], [("pb", par)])
                mm(mb[0], onesB, pb[:, par, :], True, True, [("pb", par), "const"], ["mb0"])
                rs, rk = Ft[:, 3 + par, 0:T], ("Ft", 3 + par)
                act(rs, mb[0], AF.Ln, ["mb0"], [rk], bias=EPS, scale=1.0 / 128)
                act(rs, rs, AF.Exp, [rk], [rk], scale=-0.5)
                tt("pool", rs, rs, sg[:, h, :], ALU.mult, [rk, ("D4", h)], [rk])
                stt(mixed[:, h, :], obank, V(V_HGN + h), rs, ALU.mult, ALU.mult, [obk, rk, "vec"], [("mixed", h)])
                yield

        ppmod[0] = 2
        _interleave_w([(hg_stream((0, 2, 4), [(mb[1], "mb1"), (mb[2], "mb2")]), 2),
                       (hg_stream((1, 3, 5), [(mb[3], "mb3"), (mb[4], "mb4")]), 2),
                       (stream_b(), 3)])
        ppmod[0] = 3

        chk("E11")
        D4K = [("D4", h) for h in range(6)]
        E4K = [("ktm", 0), ("ktm", 1)]
        for j in range(NB):
            js = slice(j * 128, (j + 1) * 128)
            for b in range(6):
                tp(mbbf[4][:, b * 128:(b + 1) * 128], xs[:, b, js], identB, [("xs", b), "const"], ["mb4"])
            if True:
                cp("act", xstm[:, j, :], mbbf[4][:, 0:768], ["mb4"], [("xstm", j)])
            if True:
                tt("dve", xdtd[:, j, :].rearrange("p (h q) -> p h q", q=64),
                   xstm[:, j, :].rearrange("p (h q) -> p h q", q=64),
                   dtd[:, j * 12:(j + 1) * 12].unsqueeze(2).to_broadcast([128, 12, 64]), ALU.mult, [("xstm", j)] + TM, D4K)
            if True:
                for g in range(4):
                    tp(mbbf[4][:, g * 128:(g + 1) * 128], bc[:, g, js], identB, [("bc", g), "const"], ["mb4"])
            if True:
                cp("act", bmtm[:, j, :], mbbf[4][:, 0:512], ["mb4"], E4K)
        chk("S1")
        YB = [(mb[2], "mb2"), (mb[3], "mb3"), (mb[4], "mb4")]

        def ssd_cb(j):
            js = slice(j * 128, (j + 1) * 128)
            for g in range(4):
                mm(mb[0][:, g * 128:(g + 1) * 128], bc[:, g, js], bc[:, 4 + g, js], True, True,
                   [("bc", g), ("bc", 4 + g)], ["mb0"])

        def ssd_chunk(j, g, cc):
            ybank, ybk = YB[(j * 4 + g) % 3]
            HK = ("hst", L, g)
            rows = slice(cc * 64, (cc + 1) * 64)
            cols = slice(j * 128 + cc * 64, j * 128 + (cc + 1) * 64)
            mm(ybank[rows, 192:384], bc[:, 4 + g, cols], hbf[:, L, g, :], True, True,
               [("bc", 4 + g), ("hbf", L, g)], [ybk])
            stb, stk = next_pp()
            stp = stb[:, 0:192]
            mm(stp, bmtm[rows, j, g * 128:(g + 1) * 128], xdtd[rows, j, g * 192:(g + 1) * 192], True, True,
               E4K + D4K, [stk])
            hv = hst[:, L, g, :]
            tt("dve", hv.rearrange("p (h q) -> p h q", q=64), hv.rearrange("p (h q) -> p h q", q=64),
               dcyb[:, cc, j * 12 + 3 * g:j * 12 + 3 * g + 3].unsqueeze(2).to_broadcast([128, 3, 64]),
               ALU.mult, [HK, "dcyb"], [HK])
            tt("dve", hv, hv, stp, ALU.add, [stk, HK], [HK])
            cp("dve", hbf[:, L, g, :], hv, [HK], [("hbf", L, g)])

        def ssd_A(j, g):
            if g == 0:
                ssd_cb(j)
            ssd_chunk(j, g, 0)
            yield
            for hh in range(3):
                h = 3 * g + hh
                idx = j * 12 + h
                dps = mb[1][:, 0:128]
                dk = "mb1"
                mm(dps, dhl[:, 0, idx:idx + 1].to_broadcast([128, 128]), mask01, True, False, ["dhl", "const"], [dk])
                mm(dps, dhl[:, 1, idx:idx + 1].to_broadcast([128, 128]), mask01, False, False, ["dhl", "const"], [dk])
                mm(dps, identB, mneg, False, True, ["const"], [dk])
                si = idx % 8
                act(seg[:, si, :], dps, AF.Exp, [dk] + TM, [("seg", si)], bias=bias_tm[:, idx:idx + 1])
                yield
                tt("dve", LTb[:, si, :], mb[0][:, g * 128:(g + 1) * 128], seg[:, si, :], ALU.mult,
                   ["mb0", ("seg", si)], [("LT", si)])
                yield

        def ssd_B(j, g):
            ybank, ybk = YB[(j * 4 + g) % 3]
            for hh in range(3):
                h = 3 * g + hh
                si = (j * 12 + h) % 8
                mm(ybank[:, hh * 64:(hh + 1) * 64], LTb[:, si, :], xstm[:, j, h * 64:(h + 1) * 64], True, False,
                   [("LT", si), ("xstm", j)], [ybk])
                mm(ybank[:, hh * 64:(hh + 1) * 64], idsk[:, L, h, :], xstm[:, j, h * 64:(h + 1) * 64], False, True,
                   ["idsk", ("xstm", j)], [ybk])
                yield
            ssd_chunk(j, g, 1)
            yield

        def ssd_C(j, g):
            ybank, ybk = YB[(j * 4 + g) % 3]
            yi = (j * 4 + g) % 2
            tmp = ytmp[:, yi, :]
            idx0 = j * 12 + 3 * g
            tt("dve", tmp.rearrange("p (h q) -> p h q", q=64), ybank[:, 192:384].rearrange("p (h q) -> p h q", q=64),
               ecum[:, idx0:idx0 + 3].unsqueeze(2).to_broadcast([128, 3, 64]), ALU.mult, [ybk] + TM, [("ytmp", yi)])
            yield
            tt("dve", tmp, tmp, ybank[:, 0:192], ALU.add, [ybk, ("ytmp", yi)], [("ytmp", yi)])
            yield
            tt("dve", ysb[:, j % 2, g * 192:(g + 1) * 192], tmp, G4[:, j, g * 192:(g + 1) * 192], ALU.mult,
               [("ytmp", yi), ("G4", j)], [("ysb", j % 2, g)])
            yield

        def ssd_post(j):
            js = slice(j * 128, (j + 1) * 128)
            jp = j % 2
            YK = [("ysb", jp, g) for g in range(4)]
            yv, ov, sv = ysb[:, jp, :], otm[:, jp, :], ssg[:, jp, :]
            OK_, SK_ = ("otm", jp), ("ssg", jp)
            for g in range(4):
                act(ov[:, g * 192:(g + 1) * 192], yv[:, g * 192:(g + 1) * 192], AF.Square, YK, [OK_, SK_],
                    accum_out=sv[:, g:g + 1])
                yield
            act(sv[:, 4:8], sv[:, 0:4], AF.Ln, [SK_], [SK_], bias=EPS, scale=1.0 / 192)
            act(sv[:, 4:8], sv[:, 4:8], AF.Exp, [SK_], [SK_], scale=-0.5)
            yield
            tt("dve", ov.rearrange("p (g q) -> p g q", q=192), yv.rearrange("p (g q) -> p g q", q=192),
               sv[:, 4:8].unsqueeze(2).to_broadcast([128, 4, 192]), ALU.mult, YK + [SK_], [OK_])
            yield
            tb, tk = next_pp()
            tbb = tb.bitcast(BF16)
            for b in range(6):
                tp(tbb[:, b * 128:(b + 1) * 128], ov[:, b * 128:(b + 1) * 128], identB, [OK_, "const"], [tk])
            yield
            for b in range(6):
                act(mixed[:, 10 + b, js], tbb[:, b * 128:(b + 1) * 128], AF.Identity, [tk, "vec"], [("mixed", 10 + b)],
                    scale=V(V_SSDN + b))
                yield

        its = [(j, g) for j in range(NB) for g in range(4)]
        n_it = len(its)
        for k in range(n_it + 3):
            tasks = []
            if k < n_it:
                tasks.append(ssd_A(*its[k]))
            if 1 <= k <= n_it:
                tasks.append(ssd_B(*its[k - 1]))
            if 2 <= k <= n_it + 1:
                tasks.append(ssd_C(*its[k - 2]))
            if k >= 3 and its[k - 3][1] == 3:
                tasks.append(ssd_post(its[k - 3][0]))
            _interleave(tasks, 4)

        chk("E12")
        for og in range(4):
            slot, wk = use_w(f"WO{og}")
            wv = slot[:, 0:4096].rearrange("p (k n) -> p k n", k=16)
            for oo in range(2):
                ob = og * 2 + oo
                bank, bk = next_pp()
                for kc in range(16):
                    mm(bank, wv[:, kc, oo * 128:(oo + 1) * 128], mixed[:, kc, :], kc == 0, kc == 15,
                       [wk, ("mixed", kc)], [bk])
                tt("dve", hT[:, ob, :], bank, hT[:, ob, :], ALU.add, [bk, ("hT", ob)], [("hT", ob)])
                cp("act", uT[:, ob, :], hT[:, ob, :], [("hT", ob)], [("uT", ob)])
        slotG, wkG = use_w("WG0")
        slotP, wkP = use_w("WPE")
        wpe = slotP[:, 0:2048].rearrange("p (j n) -> p j n", j=2)
        for half in range(2):
            if half == 1:
                slotG, wkG = use_w("WG1")
            wvg = wview(slotG, 512)
            for oo in range(4):
                ob = half * 4 + oo
                bankg, bkg = next_pp()
                for c in range(8):
                    mm(bankg, wvg[:, c, oo * 128:(oo + 1) * 128], uT[:, c, :], c == 0, c == 7, [wkG, ("uT", c)], [bkg])
                bankp, bkp = next_pp()
                for jj in range(2):
                    mm(bankp, wpe[:, jj, ob * 128:(ob + 1) * 128], pT[:, jj, :], jj == 0, jj == 1, [wkP, ("pT", jj)], [bkp])
                r = ob % 3
                tg, tk = Ft[:, r, 0:T], ("Ft", r)
                act(tg, bankg, AF.Tanh, [bkg], [tk], scale=0.5)
                stt(tg, tg, 1.0, bankp, ALU.add, ALU.mult, [tk, bkp], [tk])
                stt(hT[:, ob, :], tg, 0.5, hT[:, ob, :], ALU.mult, ALU.add, [tk, ("hT", ob)], [("hT", ob)])

        if dbg is not None and dbg == (it, L):
            dma("sp", dbg_d, hT, [("hT", c) for c in range(8)], ["dbgd"], "dbg")

        if last_layer:
            rms_stats(8, 1.0 / D)
            for c in range(8):
                stt(hT[:, c, :], hT[:, c, :], V(V_FNW + c), Ft[:, 5, 0:T], ALU.mult, ALU.mult,
                    [("hT", c), ("Ft", 5), "vec"], [("hT", c)])
            for j in range(NB):
                js = slice(j * 128, (j + 1) * 128)
                for cq in range(2):
                    bank, bk = next_pp()
                    for c4 in range(4):
                        c = cq * 4 + c4
                        tp(bank[:, c4 * 128:(c4 + 1) * 128], hT[:, c, js], identF, [("hT", c), "const"], [bk])
                    cp("act" if (j + cq) % 2 == 0 else "dve", iof[:, j, cq * 512:(cq + 1) * 512], bank, [bk], MIXK)
            dma("sp", out_d[t0:t0 + T, :].rearrange("(j p) d -> p j d", p=128), iof, MIXK, ["outd"], "out")

    try:
        for it in range(n_tiles):
            for L in range(n_layers):
                tile_layer(it, L, L == n_layers - 1)
    except _Stop:
        pass

    finals = [s for s in ("out", "dbg") if s in pr.dma_count]
    pr.emit(final_waits=finals)
    return nc


_CACHE = {}


def kernel(x, p, norm_w, w_in, hg_lb, hg_norm_w, pool_w, pool_scale, conv_w, conv_b, dt_bias, a_log, d_skip,
           ssd_norm_w, w_out, w_pe, w_pg, final_norm_w):
    f = lambda a: np.ascontiguousarray(np.asarray(a, dtype=np.float32))
    x, p = f(x), f(p)
    wcat = _prep_weights(f(w_in), f(w_out), f(w_pe), f(w_pg))
    vecs = _prep_vecs(f(norm_w), f(hg_lb), f(hg_norm_w), f(pool_scale), f(conv_w), f(conv_b), f(dt_bias), f(a_log),
                      f(d_skip), f(ssd_norm_w), f(final_norm_w))
    poolw = np.ascontiguousarray(f(pool_w).transpose(0, 2, 1, 3).reshape(2, 128, 512))
    if "nc" not in _CACHE:
        _CACHE["nc"] = build_program()
    nc = _CACHE["nc"]
    B = x.shape[0]
    in_maps = [{"x": x[b], "p": np.ascontiguousarray(p[:, b]), "wcat": wcat, "vecs": vecs, "poolw": poolw}
               for b in range(B)]
    res = run_bass_kernel_spmd(nc, in_maps, core_ids=list(range(B)))
    return np.stack([r["out"] for r in res.results], axis=0).astype(np.float32)
```

```python
import numpy as np
import concourse.bass as bass
import concourse.mybir as mybir
from concourse.bass_utils import run_bass_kernel_spmd

F32 = mybir.dt.float32
BF16 = mybir.dt.bfloat16
AF = mybir.ActivationFunctionType
ALU = mybir.AluOpType

S_LEN = 4096
D = 1024
T = 512
NB = T // 128
NCH = T // 64
NT = S_LEN // T
EPS = 1e-6
POOL_W = (2, 4, 8, 16)
FW = 528

_o = np.cumsum([0, 768, 768, 768, 768, 512, 512, 1792, 12, 768])
O_Q, O_F, O_I, O_G, O_UP, O_GP, O_XBC, O_DT, O_Z = [int(v) for v in _o[:9]]


def _rng(a, n):
    return list(range(a, a + n))


def _in_groups():
    g = []
    g.append(("V0", _rng(O_I, 384)))
    g.append(("V1", _rng(O_I + 384, 384)))
    g.append(("DT", _rng(O_DT, 12) + [-1] * (384 - 12)))
    for hp in range(3):
        cols = []
        for h in (2 * hp, 2 * hp + 1):
            cols += _rng(O_F + 128 * h, 128) + _rng(O_Q + 128 * h, 128)
        g.append((f"FQ{hp}", cols))
    g.append(("G0", _rng(O_G, 384)))
    g.append(("P0", _rng(O_UP, 256) + _rng(O_GP, 256)))
    g.append(("G1", _rng(O_G + 384, 384)))
    g.append(("P1", _rng(O_UP + 256, 256) + _rng(O_GP + 256, 256)))
    g.append(("SX0", _rng(O_XBC, 512)))
    g.append(("SX1", _rng(O_XBC + 512, 256) + [-1] * 128))
    g.append(("SB", _rng(O_XBC + 768, 512)))
    g.append(("SC", _rng(O_XBC + 1280, 512)))
    g.append(("Z0", _rng(O_Z, 384)))
    g.append(("Z1", _rng(O_Z + 384, 384)))
    return g


IN_GROUPS = _in_groups()
GROUP_ORDER = [n for n, _ in IN_GROUPS] + ["WO0", "WO1", "WO2", "WO3", "WG0", "WPE", "WG1"]
GROUP_SIZE = {n: 8 * len(c) for n, c in IN_GROUPS}
for _n in ("WO0", "WO1", "WO2", "WO3", "WG0", "WG1"):
    GROUP_SIZE[_n] = 4096
GROUP_SIZE["WPE"] = 2048
GROUP_OFF = {}
_acc = 0
for _n in GROUP_ORDER:
    GROUP_OFF[_n] = _acc
    _acc += GROUP_SIZE[_n]
NW = _acc

NV = 150
V_NW, V_HGN, V_LBA, V_LBB, V_PSC, V_CW, V_CB, V_SSDN, V_FNW, V_DTB, V_ALOG, V_DSK = 0, 8, 14, 20, 26, 30, 86, 100, 106, 114, 126, 138


def _prep_weights(w_in, w_out, w_pe, w_pg):
    L = w_in.shape[0]
    out = np.zeros((L, 128, NW), np.float32)
    for l in range(L):
        wi = w_in[l].reshape(8, 128, -1)
        for name, cols in IN_GROUPS:
            n = len(cols)
            blk = np.zeros((128, 8, n), np.float32)
            idx = np.array([c for c in cols if c >= 0])
            blk[:, :, : len(idx)] = wi[:, :, idx].transpose(1, 0, 2)
            out[l, :, GROUP_OFF[name]: GROUP_OFF[name] + 8 * n] = blk.reshape(128, -1)
        wo = w_out[l].reshape(16, 128, 1024)
        for g in range(4):
            blk = wo[:, :, g * 256:(g + 1) * 256].transpose(1, 0, 2)
            out[l, :, GROUP_OFF[f"WO{g}"]: GROUP_OFF[f"WO{g}"] + 4096] = blk.reshape(128, -1)
        wg = w_pg[l].reshape(8, 128, 1024)
        for g in range(2):
            blk = wg[:, :, g * 512:(g + 1) * 512].transpose(1, 0, 2)
            out[l, :, GROUP_OFF[f"WG{g}"]: GROUP_OFF[f"WG{g}"] + 4096] = blk.reshape(128, -1)
        wp = w_pe[l].reshape(2, 128, 1024).transpose(1, 0, 2)
        out[l, :, GROUP_OFF["WPE"]: GROUP_OFF["WPE"] + 2048] = wp.reshape(128, -1)
    return out


def _prep_vecs(norm_w, hg_lb, hg_norm_w, pool_scale, conv_w, conv_b, dt_bias, a_log, d_skip, ssd_norm_w, final_norm_w):
    L = norm_w.shape[0]
    v = np.zeros((L, 128, NV), np.float32)
    for l in range(L):
        v[l, :, V_NW:V_NW + 8] = norm_w[l].reshape(8, 128).T
        v[l, :, V_HGN:V_HGN + 6] = hg_norm_w[l].reshape(6, 128).T
        v[l, :, V_LBA:V_LBA + 6] = hg_lb[0].reshape(6, 128).T
        v[l, :, V_LBB:V_LBB + 6] = hg_lb[l].reshape(6, 128).T
        v[l, :, V_PSC:V_PSC + 4] = pool_scale[l].reshape(4, 128).T
        cw = conv_w[l].reshape(4, 14, 128)
        v[l, :, V_CW:V_CW + 56] = cw.transpose(2, 0, 1).reshape(128, 56)
        v[l, :, V_CB:V_CB + 14] = conv_b[l].reshape(14, 128).T
        v[l, :, V_SSDN:V_SSDN + 6] = ssd_norm_w[l].reshape(6, 128).T
        v[l, :, V_FNW:V_FNW + 8] = final_norm_w.reshape(8, 128).T
        v[l, :, V_DTB:V_DTB + 12] = dt_bias[l][None, :]
        v[l, :, V_ALOG:V_ALOG + 12] = a_log[l][None, :]
        v[l, :, V_DSK:V_DSK + 12] = d_skip[l][None, :]
    return v


class _Op:
    __slots__ = ("eng", "fn", "deps", "dma_slot", "idx", "sig", "sigval", "waits", "multi", "raw", "epos")


class Prog:
    ENGS = ("pe", "act", "dve", "pool", "sp")
    DEPTH = {"pe": 0, "act": 2, "dve": 2, "pool": 8, "sp": 0}

    def __init__(self, nc):
        self.nc = nc
        self.ops = []
        self.state = {}
        self.dma_count = {}

    def add(self, eng, fn, R=(), W=(), dma_slot=None, multi=False):
        op = _Op()
        op.eng, op.fn, op.dma_slot, op.multi = eng, fn, dma_slot, multi
        op.idx = len(self.ops)
        op.sig = False
        op.sigval = None
        deps = {}
        raw = set()
        for k in R:
            st = self.state.get(k)
            if st is not None and st[0] is not None:
                deps[st[0].idx] = st[0]
                raw.add(st[0].idx)
        op.raw = raw
        for k in W:
            st = self.state.get(k)
            if st is not None:
                if st[0] is not None:
                    deps[st[0].idx] = st[0]
                for r in st[1]:
                    deps[r.idx] = r
        deps.pop(op.idx, None)
        op.deps = list(deps.values())
        for k in R:
            st = self.state.setdefault(k, [None, []])
            st[1].append(op)
        for k in W:
            self.state[k] = [op, []]
        if dma_slot is not None:
            c = self.dma_count.get(dma_slot, 0) + 1
            self.dma_count[dma_slot] = c
            op.sig = True
            op.sigval = 16 * c
        self.ops.append(op)
        return op

    def emit(self, final_waits=()):
        nc = self.nc
        epos = {e: 0 for e in self.ENGS}
        for op in self.ops:
            op.epos = epos[op.eng]
            epos[op.eng] += 1

        def same_eng_needs(op, d):
            return (op.eng != "pe" and d.dma_slot is None and op.dma_slot is None
                    and op.epos - d.epos <= self.DEPTH[op.eng])

        for op in self.ops:
            for d in op.deps:
                if d.dma_slot is None and (d.eng != op.eng or op.dma_slot is not None or same_eng_needs(op, d)):
                    d.sig = True
        cnt = {e: 0 for e in self.ENGS}
        for op in self.ops:
            if op.dma_slot is None and op.sig:
                cnt[op.eng] += 1
                op.sigval = cnt[op.eng]
        known = {e: {} for e in self.ENGS}
        snap = {}
        for op in self.ops:
            kn = known[op.eng]
            waits = []
            for d in op.deps:
                if d.dma_slot is None:
                    if d.eng == op.eng and op.dma_slot is None and not same_eng_needs(op, d):
                        continue
                    key = "E_" + d.eng
                else:
                    key = "D_" + d.dma_slot
                if kn.get(key, 0) >= d.sigval:
                    continue
                waits.append((key, d.sigval))
                kn[key] = d.sigval
                sn = snap.get(d.idx)
                if sn is not None:
                    for k2, v2 in sn.items():
                        if kn.get(k2, 0) < v2:
                            kn[k2] = v2
            wm = {}
            for k, v in waits:
                wm[k] = max(wm.get(k, 0), v)
            op.waits = list(wm.items())
            if op.sig and op.dma_slot is None:
                kn2 = dict(kn)
                snap[op.idx] = kn2
        sem_names = ["E_" + e for e in self.ENGS] + ["D_" + s for s in self.dma_count]
        from contextlib import ExitStack
        with ExitStack() as es:
            sems = {}
            for n in sem_names:
                sems[n] = es.enter_context(nc.semaphore(n))
            block = es.enter_context(nc.Block())
            byeng = {e: [o for o in self.ops if o.eng == e] for e in self.ENGS}

            def run(ename, eh):
                for op in byeng[ename]:
                    ws = op.waits
                    if op.multi or len(ws) > 1 or ename in ("pe", "sp") or op.dma_slot is not None:
                        nattach = 0 if (op.multi or ename in ("pe", "sp") or op.dma_slot is not None) else 1
                        for k, v in ws[nattach:]:
                            eh.wait_ge(sems[k], v)
                        ws = ws[:nattach]
                    ins = op.fn(eh)
                    for k, v in ws:
                        ins._wait_ge(sems[k], v)
                    if op.sig:
                        if op.dma_slot is not None:
                            ins.then_inc(sems["D_" + op.dma_slot], 16)
                        else:
                            ins.then_inc(sems["E_" + ename], 1)
                if ename == "sp":
                    for slot in final_waits:
                        eh.wait_ge(sems["D_" + slot], 16 * self.dma_count[slot])

            @block.tensor
            def _(e):
                run("pe", e)

            @block.scalar
            def _(e):
                run("act", e)

            @block.vector
            def _(e):
                run("dve", e)

            @block.gpsimd
            def _(e):
                run("pool", e)

            @block.sync
            def _(e):
                run("sp", e)


class _Stop(Exception):
    pass


def _interleave_gen(tasks, width):
    tasks = iter(tasks)
    live = []
    done = False
    while True:
        while not done and len(live) < width:
            try:
                live.append(next(tasks))
            except StopIteration:
                done = True
        if not live:
            break
        for g in list(live):
            try:
                next(g)
            except StopIteration:
                live.remove(g)
            yield


def _interleave(tasks, width):
    tasks = iter(tasks)
    live = []
    done = False
    while True:
        while not done and len(live) < width:
            try:
                live.append(next(tasks))
            except StopIteration:
                done = True
        if not live:
            break
        for g in list(live):
            try:
                next(g)
            except StopIteration:
                live.remove(g)


def build_program(n_tiles=NT, n_layers=2, dbg=None, stop=None):
    nc = bass.Bass("TRN2", target_bir_lowering=False)
    pr = Prog(nc)
    x_d = nc.dram_tensor("x", [S_LEN, D], F32, kind="ExternalInput").ap()
    p_d = nc.dram_tensor("p", [2, S_LEN, 256], F32, kind="ExternalInput").ap()
    wcat_d = nc.dram_tensor("wcat", [2, 128, NW], F32, kind="ExternalInput").ap()
    vecs_d = nc.dram_tensor("vecs", [2, 128, NV], F32, kind="ExternalInput").ap()
    poolw_d = nc.dram_tensor("poolw", [2, 128, 512], F32, kind="ExternalInput").ap()
    out_d = nc.dram_tensor("out", [S_LEN, D], F32, kind="ExternalOutput").ap()
    wbf_d = nc.dram_tensor("wbf", [2, 128, NW], BF16, kind="Internal").ap()
    dbg_d = None
    if dbg is not None:
        dbg_d = nc.dram_tensor("dbg", [128, 8, T], F32, kind="ExternalOutput").ap()

    def sb(name, shape, dt):
        return nc.alloc_sbuf_tensor(name, shape, dt).ap()

    hT = sb("hT", [128, 8, T], F32)
    uT = sb("uT", [128, 8, T], BF16)
    mixraw = sb("mixraw", [128, 16 * T], BF16)
    mixed = mixraw.rearrange("p (k t) -> p k t", t=T)
    iof = mixraw.bitcast(F32).rearrange("p (j d) -> p j d", d=D)
    MIXK = [("mixed", k) for k in range(16)]
    pin = sb("pin", [128, NB, 256], F32)
    pT = sb("pT", [128, 2, T], BF16)
    wsl = [sb(f"wsl{i}", [128, 4096], BF16) for i in range(4)]
    Sst = sb("Sst", [128, 2, 6, 128], F32)
    hst = sb("hst", [128, 2, 4, 192], F32)
    hbf = sb("hbf", [128, 2, 4, 192], BF16)
    phalo = sb("phalo", [128, 2, 4, 16], F32)
    chalo = sb("chalo", [128, 2, 14, 4], F32)
    vec = sb("vec", [128, 2, NV], F32)
    lbv = sb("lbv", [128, 2, 3, 6], F32)
    a_b = sb("a_b", [128, 2, 12], F32)
    idsk = sb("idsk", [128, 2, 12, 128], BF16)
    pwb = sb("pwb", [128, 2, 4, 128], BF16)
    cfx = sb("cfx", [128, 4, 16], F32)
    identF = sb("identF", [128, 128], F32)
    identB = sb("identB", [128, 128], BF16)
    onesB = sb("onesB", [128, 128], BF16)
    mask01 = sb("mask01", [128, 128], BF16)
    mneg = sb("mneg", [128, 128], BF16)
    tri = sb("tri", [128, 128], F32)
    ind = sb("ind", [128, 2, 128], F32)
    smask = sb("smask", [128, T], F32)
    qT = sb("qT", [128, 6, T], BF16)
    kT = sb("kT", [128, 6, T], BF16)
    vtm = sb("vtm", [128, NB, 768], BF16)
    xs = sb("xs", [128, 6, T], BF16)
    bc = sb("bc", [128, 8, T], BF16)
    xstm = sb("xstm", [128, NB, 768], BF16)
    D4 = sb("D4", [128, 3072], BF16)
    sg = D4.rearrange("p (h t) -> p h t", t=T)
    xdtd = D4.rearrange("p (j n) -> p j n", n=768)
    E4 = sb("E4", [128, 2048], BF16)
    ktm = E4[:, 0:1024].rearrange("p (a j k) -> p a j k", a=2, j=4)
    bmtm = E4.rearrange("p (j n) -> p j n", n=512)
    Ft = sb("Ft", [128, 6, FW], F32)
    G4 = sb("G4", [128, NB, 768], BF16)
    pb = sb("pb", [128, 4, T], BF16)
    AT = sb("AT", [128, 2, 4, 128], BF16)
    Sp = sb("Sp", [128, 2, 8, 128], BF16)
    smH = sb("smH", [128, 6, 3, 8], F32)
    seg = sb("seg", [128, 8, 128], F32)
    LTb = sb("LTb", [128, 8, 128], BF16)
    ysb = sb("ysb", [128, 2, 768], F32)
    otm = sb("otm", [128, 2, 768], BF16)
    tmS = sb("tmS", [128, 9, 48], F32)
    dcyb = sb("dcyb", [128, 2, 48], F32)
    ssg = sb("ssg", [128, 2, 8], F32)
    dhl = sb("dhl", [128, 2, 48], BF16)
    ytmp = sb("ytmp", [128, 2, 192], F32)

    pp = [nc.alloc_psum_tensor(f"pp{i}", [128, 512], F32).ap() for i in range(3)]
    mb = [nc.alloc_psum_tensor(f"mb{i}", [128, 512], F32).ap() for i in range(5)]
    mbbf = [m.bitcast(BF16) for m in mb]
    ppi = [0]
    MB0 = [("mb0", i) for i in range(4)]
    MB2 = [("mb2", i) for i in range(4)]

    ppmod = [3]

    def next_pp():
        i = ppi[0] % ppmod[0]
        ppi[0] += 1
        return pp[i], f"pp{i}"

    def act(out, in_, func, R, W, bias=0.0, scale=1.0, accum_out=None):
        if accum_out is None:
            pr.add("act", lambda e: e.activation(out=out, in_=in_, func=func, bias=bias, scale=scale), R, W)
        else:
            pr.add("act", lambda e: e.activation(out=out, in_=in_, func=func, bias=bias, scale=scale,
                                                 accum_out=accum_out), R, W, multi=True)

    def tt(eng, out, in0, in1, op, R, W):
        pr.add(eng, lambda e: e.tensor_tensor(out=out, in0=in0, in1=in1, op=op), R, W)

    def ts(eng, out, in0, s1, s2, op0, op1, R, W):
        if op1 is None:
            pr.add(eng, lambda e: e.tensor_scalar(out=out, in0=in0, scalar1=s1, scalar2=None, op0=op0), R, W)
        else:
            pr.add(eng, lambda e: e.tensor_scalar(out=out, in0=in0, scalar1=s1, scalar2=s2, op0=op0, op1=op1), R, W)

    def stt(out, in0, scalar, in1, op0, op1, R, W):
        pr.add("dve", lambda e: e.scalar_tensor_tensor(out=out, in0=in0, scalar=scalar, in1=in1, op0=op0, op1=op1), R, W)

    def cp(eng, out, in_, R, W):
        if eng == "act":
            pr.add("act", lambda e: e.activation(out=out, in_=in_, func=AF.Copy), R, W)
        else:
            pr.add(eng, lambda e: e.tensor_copy(out=out, in_=in_), R, W)

    def mm(out, lhsT, rhs, start, stop, R, W):
        pr.add("pe", lambda e: e.matmul(out, lhsT=lhsT, rhs=rhs, start=start, stop=stop), R, W)

    def tp(out, in_, ident, R, W):
        pr.add("pe", lambda e: e.transpose(out, in_, ident), R, W)

    def dma(q, out, in_, R, W, slot):
        pr.add(q, lambda e: e.dma_start(out=out, in_=in_), R, W, dma_slot=slot)

    def memset(eng, ap, val, W):
        pr.add(eng, lambda e: e.memset(ap, val), (), W)

    for L in range(2):
        dma("sp", vec[:, L, :], vecs_d[L], (), ("vec",), "vec")
    ncv = 0
    for L in range(n_layers):
        for gname in GROUP_ORDER:
            off, sz = GROUP_OFF[gname], GROUP_SIZE[gname]
            src = wcat_d[L][:, off:off + sz].rearrange("p (k n) -> p k n", n=1024)
            dst = wbf_d[L][:, off:off + sz].rearrange("p (k n) -> p k n", n=1024)
            rk = [("cvtok", ncv - 4)] if ncv >= 4 else []
            dma("pool", dst, src, rk, [("wbf", L, gname), ("cvtok", ncv)], f"cv{L}{gname}")
            ncv += 1
    for L in range(2):
        dma("pool", pwb[:, L, :, :].rearrange("p g d -> p (g d)"), poolw_d[L], (), ("pwb",), "pwb")
    memset("pool", Sst, 0.0, [("Sst", L, h) for L in range(2) for h in range(6)])
    memset("pool", hst, 0.0, [("hst", L, g) for L in range(2) for g in range(4)])
    memset("pool", hbf, 0.0, [("hbf", L, g) for L in range(2) for g in range(4)])
    memset("pool", phalo, 0.0, ["phalo"])
    memset("pool", chalo, 0.0, ["chalo"])
    memset("pool", onesB, 1.0, ["const"])
    memset("pool", identF, 1.0, ["const"])
    pr.add("pool", lambda e: e.affine_select(out=identF, in_=identF, pattern=[[1, 128]], compare_op=ALU.is_ge,
                                             fill=0.0, base=0, channel_multiplier=-1), ["const"], ["const"])
    cp("pool", tri, identF, ["const"], ["const"])
    pr.add("pool", lambda e: e.affine_select(out=identF, in_=identF, pattern=[[-1, 128]], compare_op=ALU.is_ge,
                                             fill=0.0, base=0, channel_multiplier=1), ["const"], ["const"])
    cp("pool", identB, identF, ["const"], ["const"])
    memset("pool", tri[0:64, 64:128], 0.0, ["const"])
    cp("pool", mask01, tri, ["const"], ["const"])
    ts("pool", mneg, tri, -1.0, 1.0e5, ALU.add, ALU.mult, ["const"], ["const"])
    memset("pool", ind, 0.0, ["const"])
    memset("pool", ind[0:64, 0, :], 1.0, ["const"])
    memset("pool", ind[64:128, 1, :], 1.0, ["const"])
    memset("pool", smask, 1.0, ["const"])
    memset("pool", smask.rearrange("p (c t) -> p c t", t=64)[:, :, 0:1], 0.0, ["const"])
    memset("pool", cfx, 1.0, ["const"])
    for g, w in enumerate(POOL_W):
        for t_ in range(w - 1):
            memset("pool", cfx[:, g, t_:t_ + 1], float(w) / float(t_ + 1), ["const"])
    memset("dve", lbv[:, 0, 0, :], 0.0, ["lbv"])
    memset("dve", lbv[:, 0, 1, :], 1.0, ["lbv"])
    memset("dve", lbv[:, 0, 2, :], -1.0, ["lbv"])
    tt("dve", lbv[:, 1, 0, :], vec[:, 1, V_LBA:V_LBA + 6], vec[:, 1, V_LBB:V_LBB + 6], ALU.subtract, ["vec"], ["lbv"])
    act(lbv[:, 1, 0, :], lbv[:, 1, 0, :], AF.Exp, ["lbv"], ["lbv"])
    ts("dve", lbv[:, 1, 0, :], lbv[:, 1, 0, :], 1.0, None, ALU.add, None, ["lbv"], ["lbv"])
    pr.add("dve", lambda e: e.reciprocal(out=lbv[:, 1, 0, :], in_=lbv[:, 1, 0, :]), ["lbv"], ["lbv"])
    ts("dve", lbv[:, 1, 1, :], lbv[:, 1, 0, :], -1.0, 1.0, ALU.mult, ALU.add, ["lbv"], ["lbv"])
    ts("dve", lbv[:, 1, 2, :], lbv[:, 1, 1, :], -1.0, None, ALU.mult, None, ["lbv"], ["lbv"])
    for L in range(2):
        act(a_b[:, L, :], vec[:, L, V_ALOG:V_ALOG + 12], AF.Exp, ["vec"], ["a_b"])
        ts("dve", a_b[:, L, :], a_b[:, L, :], -1.0, None, ALU.mult, None, ["a_b"], ["a_b"])
        for h in range(12):
            ts("dve", idsk[:, L, h, :], identF, vec[:, L, V_DSK + h:V_DSK + h + 1], None, ALU.mult, None,
               ["const", "vec"], ["idsk"])

    wseq = []
    for it in range(n_tiles):
        for L in range(n_layers):
            for gname in GROUP_ORDER:
                wseq.append((L, gname))
    wstate = {"issued": 0, "used": 0}

    def use_w(expect):
        k = wstate["used"]
        while wstate["issued"] < min(len(wseq), k + 3):
            j = wstate["issued"]
            L_, g_ = wseq[j]
            off, sz = GROUP_OFF[g_], GROUP_SIZE[g_]
            s = j % 4
            dma("sp", wsl[s][:, 0:sz], wbf_d[L_][:, off:off + sz], [("wbf", L_, g_)], [("wsl", s)], f"w{s}")
            wstate["issued"] += 1
        wstate["used"] += 1
        assert wseq[k][1] == expect, (wseq[k], expect)
        return wsl[k % 4], ("wsl", k % 4)

    def wview(slot, n):
        return slot[:, 0:8 * n].rearrange("p (c n) -> p c n", c=8)

    def tile_layer(it, L, last_layer):
        t0 = it * T

        def V(a, n=1):
            return vec[:, L, a:a + n]

        def chk(name):
            if stop == name and it == 0 and L == 0:
                dma("sp", dbg_d, hT, [("hT", c) for c in range(8)], ["dbgd"], "dbg")
                raise _Stop()

        def rms_stats(nchunk, scale):
            bank, bk = next_pp()
            for c in range(nchunk):
                act(uT[:, c, :], hT[:, c, :], AF.Square, [("hT", c)], [("uT", c)])
            for c in range(nchunk):
                mm(bank, onesB, uT[:, c, :], c == 0, c == nchunk - 1, [("uT", c), "const"], [bk])
            act(Ft[:, 5, 0:T], bank, AF.Ln, [bk], [("Ft", 5)], bias=EPS, scale=scale)
            act(Ft[:, 5, 0:T], Ft[:, 5, 0:T], AF.Exp, [("Ft", 5)], [("Ft", 5)], scale=-0.5)

        if L == 0:
            dma("sp", iof, x_d[t0:t0 + T, :].rearrange("(j p) d -> p j d", p=128), (), MIXK, "xin")
            for c in range(8):
                bank, bk = next_pp()
                for j in range(NB):
                    tp(bank[:, j * 128:(j + 1) * 128], iof[:, j, c * 128:(c + 1) * 128], identF, MIXK + ["const"], [bk])
                cp("act" if c % 2 == 0 else "dve", hT[:, c, :], bank, [bk], [("hT", c)])
        dma("sp", pin, p_d[L, t0:t0 + T, :].rearrange("(j p) d -> p j d", p=128), (), ["pin"], "pin")
        for jj in range(2):
            bank, bk = next_pp()
            for j in range(NB):
                tp(bank[:, j * 128:(j + 1) * 128], pin[:, j, jj * 128:(jj + 1) * 128], identF, ["pin", "const"], [bk])
            cp("act", pT[:, jj, :], bank, [bk], [("pT", jj)])

        chk("E1")
        rms_stats(8, 1.0 / D)
        for c in range(8):
            stt(uT[:, c, :], hT[:, c, :], V(V_NW + c), Ft[:, 5, 0:T], ALU.mult, ALU.mult,
                [("hT", c), ("Ft", 5), "vec"], [("uT", c)])

        def proj_fm(wv, col0, bank, bk, wk):
            for c in range(8):
                mm(bank, wv[:, c, col0:col0 + 128], uT[:, c, :], c == 0, c == 7, [wk, ("uT", c)], [bk])

        def proj_tm(wv, ncols, j, bank, bk, wk):
            for c in range(8):
                mm(bank[:, 0:ncols], uT[:, c, j * 128:(j + 1) * 128], wv[:, c, 0:ncols], c == 0, c == 7,
                   [wk, ("uT", c)], [bk])

        chk("E2")
        for half in range(2):
            slot, wk = use_w(f"V{half}")
            wv = wview(slot, 384)
            for j in range(NB):
                bank, bk = next_pp()
                proj_tm(wv, 384, j, bank, bk, wk)
                cp("act", vtm[:, j, half * 384:(half + 1) * 384], bank[:, 0:384], [bk], [("vtm", j)])

        chk("E3")
        slot, wk = use_w("DT")
        wv = wview(slot, 384)
        sbk = "mb4"
        for j in range(NB):
            for c in range(8):
                mm(mb[4][:, j * 12:(j + 1) * 12], uT[:, c, j * 128:(j + 1) * 128], wv[:, c, 0:12], c == 0, c == 7,
                   [wk, ("uT", c)], [sbk])
        dt_raw, dt_tm, lndt, dta, cum_tm, bias_tm, ecum, d_tm, dtd = (tmS[:, i, :] for i in range(9))
        TM = ["tmS"]

        def r3(a):
            return a.rearrange("p (j h) -> p j h", h=12)

        def dt_task():
            tt("dve", r3(dt_raw), r3(mb[4][:, 0:48]), V(V_DTB, 12).unsqueeze(1).to_broadcast([128, NB, 12]), ALU.add,
               [sbk, "vec"], TM)
            yield
            act(dt_tm, dt_raw, AF.Exp, TM, TM)
            yield
            act(dt_tm, dt_tm, AF.Ln, TM, TM, bias=1.0)
            yield
            act(lndt, dt_tm, AF.Ln, TM, TM)
            yield
            tt("dve", r3(dta), r3(dt_tm), a_b[:, L, :].unsqueeze(1).to_broadcast([128, NB, 12]), ALU.mult, TM + ["a_b"], TM)
            yield
            cp("dve", dhl[:, 0, :], dta, TM, ["dhl"])
            yield
            tt("dve", dhl[:, 1, :], dta, dhl[:, 0, :], ALU.subtract, TM + ["dhl"], ["dhl"])
            yield
            mm(mb[4][:, 64:112], tri, dta, True, True, TM + ["const"], [sbk])
            yield
            for c in range(2):
                mm(mb[4][:, 128 + 48 * c:176 + 48 * c], ind[:, c, :], dta, True, True, TM + ["const"], [sbk])
            yield
            cp("dve", cum_tm, mb[4][:, 64:112], [sbk], TM)
            yield
            tt("dve", bias_tm, lndt, cum_tm, ALU.subtract, TM, TM)
            yield
            act(ecum, cum_tm, AF.Exp, TM, TM)
            yield
            act(dcyb.rearrange("p c n -> p (c n)"), mb[4][:, 128:224], AF.Exp, [sbk], ["dcyb"])
            yield
            for c in range(2):
                tt("dve", d_tm[c * 64:(c + 1) * 64, :], mb[4][c * 64:(c + 1) * 64, 128 + 48 * c:176 + 48 * c],
                   cum_tm[c * 64:(c + 1) * 64, :], ALU.subtract, [sbk] + TM, TM)
            yield
            act(d_tm, d_tm, AF.Exp, TM, TM)
            yield
            tt("dve", dtd, d_tm, dt_tm, ALU.mult, TM, TM)
            yield

        def fq_task(h, hh, wv, wk):
            fs = 3 * (h % 2)
            t1, t2, t3 = Ft[:, fs, 0:T], Ft[:, fs + 1, 0:T], Ft[:, fs + 2, 0:T]
            k1, k2, k3 = ("Ft", fs), ("Ft", fs + 1), ("Ft", fs + 2)
            smk = ("smH", h)
            lb_, oml_, noml_ = lbv[:, L, 0, h:h + 1], lbv[:, L, 1, h:h + 1], lbv[:, L, 2, h:h + 1]
            bank, bk = next_pp()
            proj_fm(wv, hh * 256, bank, bk, wk)
            act(t1, bank, AF.Exp, [bk], [k1], scale=-1.0)
            yield
            act(t1, t1, AF.Ln, [k1], [k1], bias=1.0)
            yield
            act(t1, t1, AF.Exp, [k1], [k1], scale=-1.0)
            yield
            act(t2, t1, AF.Ln, [k1, "lbv"], [k2], bias=lb_, scale=oml_)
            yield
            ts("dve", t1, t1, noml_, oml_, ALU.mult, ALU.add, [k1, "lbv"], [k1])
            pr.add("dve", lambda e, t3=t3, t2=t2: e.tensor_tensor_scan(out=t3, data0=smask, data1=t2, initial=0.0,
                                                                       op0=ALU.mult, op1=ALU.add),
                   [k2, "const"], [k3])
            yield
            c3 = t3.rearrange("p (c t) -> p c t", t=64)
            act(smH[:, h, 0, :], c3[:, :, 31], AF.Exp, [k3], [smk])
            act(smH[:, h, 1, :], c3[:, :, 63], AF.Exp, [k3], [smk])
            tt("dve", t2.rearrange("p (c t) -> p c t", t=64), c3, c3[:, :, 31:32].to_broadcast([128, NCH, 64]),
               ALU.subtract, [k3], [k2])
            yield
            act(t3, t2, AF.Exp, [k2], [k3])
            act(t2, t2, AF.Exp, [k2], [k2], scale=-1.0)
            cp("act", smH[:, h, 2, :], t3.rearrange("p (c t) -> p c t", t=64)[:, :, 63], [k3], [smk])
            bank2, bk2 = next_pp()
            proj_fm(wv, hh * 256 + 128, bank2, bk2, wk)
            yield
            tt("dve", qT[:, h, :], bank2, t3, ALU.mult, [bk2, k3], [("qT", h)])
            tt("pool", kT[:, h, :], t1, t2, ALU.mult, [k1, k2], [("kT", h)])

        def fq_all():
            for hp in range(3):
                slot, wk = use_w(f"FQ{hp}")
                wv = wview(slot, 512)
                yield from _interleave_gen([fq_task(2 * hp + hh, hh, wv, wk) for hh in range(2)], 2)

        _interleave([fq_all(), dt_task()], 2)

        chk("E4")
        chk("E5")
        def g_all():
            for half in range(2):
                slot, wk = use_w(f"G{half}")
                wv = wview(slot, 384)
                for hh in range(3):
                    h = half * 3 + hh
                    bank, bk = next_pp()
                    proj_fm(wv, hh * 128, bank, bk, wk)
                    yield
                    act(sg[:, h, :], bank, AF.Silu, [bk], [("D4", h)])
                    yield

        def pool_task(g, gg, wv, wk):
            w = POOL_W[g]
            ub, la, lb2 = Ft[:, 3 * gg, :], Ft[:, 3 * gg + 1, :], Ft[:, 3 * gg + 2, :]
            ku, ka, kb = ("Ft", 3 * gg), ("Ft", 3 * gg + 1), ("Ft", 3 * gg + 2)
            NN = 16 + T
            bank, bk = next_pp()
            proj_fm(wv, gg * 128, bank, bk, wk)
            cp("pool", ub[:, 0:16], phalo[:, L, g, :], ["phalo"], [ku])
            yield
            cp("act", ub[:, 16:NN], bank, [bk], [ku])
            yield
            bank2, bk2 = next_pp()
            proj_fm(wv, 256 + gg * 128, bank2, bk2, wk)
            yield
            act(pb[:, gg, :], bank2, AF.Silu, [bk2], [("pb", gg)])
            cp("pool", phalo[:, L, g, :], ub[:, T:NN], [ku], ["phalo"])
            yield
            src_, sk = ub, ku
            dsts = [(la, ka), (lb2, kb)]
            step, lvl = 1, 0
            while step < w:
                dst, dk = dsts[lvl % 2]
                lo = 2 * step - 1
                tt("pool", dst[:, lo:NN], src_[:, lo:NN], src_[:, lo - step:NN - step], ALU.add, [sk], [dk])
                yield
                src_, sk = dst, dk
                step *= 2
                lvl += 1
            if it == 0:
                tt("pool", src_[:, 16:32], src_[:, 16:32], cfx[:, g, :], ALU.mult, [sk, "const"], [sk])
            stt(pb[:, 2 + gg, :], src_[:, 16:NN], 1.0 / w, ub[:, 16:NN], ALU.mult, ALU.subtract,
                [ku, sk], [("pb", 2 + gg)])
            yield
            mm(mb[0], pwb[:, L, g, :], pb[:, 2 + gg, :], True, True, ["pwb", ("pb", 2 + gg)], ["mb0"])
            stt(mixed[:, 6 + g, :], mb[0], V(V_PSC + g), pb[:, gg, :], ALU.mult, ALU.mult,
                ["mb0", ("pb", gg), "vec"], [("mixed", 6 + g)])
            yield

        def pool_all():
            for half in range(2):
                slot, wk = use_w(f"P{half}")
                wv = wview(slot, 512)
                yield from _interleave_gen([pool_task(half * 2 + gg, gg, wv, wk) for gg in range(2)], 2)

        _interleave([g_all(), pool_all()], 2)

        chk("E7")
        rawctr = [0]

        def conv_task(b, bi, wv, wk):
            r = rawctr[0] % 3
            rawctr[0] += 1
            raw, kr = Ft[:, r, :], ("Ft", r)
            bank, bk = next_pp()
            proj_fm(wv, bi * 128, bank, bk, wk)
            cp("pool", raw[:, 0:3], chalo[:, L, b, 0:3], ["chalo"], [kr])
            cp("act", raw[:, 3:3 + T], bank, [bk], [kr])
            act(bank, bank, AF.Identity, [bk, "vec"], [bk], bias=V(V_CB + b), scale=V(V_CW + 3 * 14 + b))
            yield
            for tap in (2, 1, 0):
                stt(bank, raw[:, tap:tap + T], V(V_CW + tap * 14 + b), bank, ALU.mult, ALU.add,
                    [kr, bk, "vec"], [bk])
                yield
            cp("pool", chalo[:, L, b, 0:3], raw[:, T:T + 3], [kr], ["chalo"])
            if b < 6:
                act(xs[:, b, :], bank, AF.Silu, [bk], [("xs", b)])
            else:
                act(bc[:, b - 6, :], bank, AF.Silu, [bk], [("bc", b - 6)])

        def conv_tasks():
            for gname, nblk, blk0, ncols in (("SX0", 4, 0, 512), ("SX1", 2, 4, 384), ("SB", 4, 6, 512), ("SC", 4, 10, 512)):
                slot, wk = use_w(gname)
                wv = wview(slot, ncols)
                for bi in range(nblk):
                    yield conv_task(blk0 + bi, bi, wv, wk)


        chk("E8")
        def z_task(half, j, wv, wk):
            bank, bk = next_pp()
            proj_tm(wv, 384, j, bank, bk, wk)
            yield
            act(G4[:, j, half * 384:(half + 1) * 384], bank[:, 0:384], AF.Silu, [bk], [("G4", j)])

        def z_tasks():
            for half in range(2):
                slot, wk = use_w(f"Z{half}")
                wv = wview(slot, 384)
                for j in range(NB):
                    yield z_task(half, j, wv, wk)

        def stream_b():
            yield from _interleave_gen(conv_tasks(), 2)
            yield from _interleave_gen(z_tasks(), 2)

        chk("E9")
        def hg_stream(heads, KVB):
            obank, obk = pp[2], "pp2"
            for h in heads:
                par = h % 2
                for j in range(NB):
                    js = slice(j * 128, (j + 1) * 128)
                    mm(mb[0][:, js], kT[:, h, js], qT[:, h, js], True, True, [("kT", h), ("qT", h)], ["mb0"])
                tt("dve", AT[:, par, :, :], mb[0].rearrange("p (j t) -> p j t", t=128),
                   mask01.unsqueeze(1).to_broadcast([128, NB, 128]), ALU.mult, ["mb0", "const"], [("AT", par)])
                yield
                for j in range(NB):
                    js = slice(j * 128, (j + 1) * 128)
                    tp(mbbf[0][:, js], kT[:, h, js], identB, [("kT", h), "const"], ["mb0"])
                cp("act", ktm[:, par, :, :].rearrange("p j k -> p (j k)"), mbbf[0][:, 0:512], ["mb0"], [("ktm", par)])
                S = Sst[:, L, h, :]
                SK = ("Sst", L, h)
                act(Sp[:, par, 0, :], S, AF.Identity, [SK, ("smH", h)], [("Sp", par, 0)], scale=smH[:, h, 0, 0:1])
                yield
                for c in range(NCH):
                    j, cc = c // 2, c % 2
                    rows = slice(cc * 64, (cc + 1) * 64)
                    kvb, kvk = KVB[cc]
                    mm(kvb[:, j * 128:(j + 1) * 128], ktm[rows, par, j, :], vtm[rows, j, h * 128:(h + 1) * 128],
                       True, True, [("ktm", par), ("vtm", j)], [kvk])
                yield
                for c in range(NCH):
                    j, cc = c // 2, c % 2
                    kvb, kvk = KVB[cc]
                    kvs = kvb[:, j * 128:(j + 1) * 128]
                    ts("dve", S, S, smH[:, h, 1, c:c + 1], None, ALU.mult, None, [SK, ("smH", h)], [SK])
                    yield
                    stt(S, kvs, smH[:, h, 2, c:c + 1], S, ALU.mult, ALU.add, [kvk, SK, ("smH", h)], [SK])
                    yield
                    if c < NCH - 1:
                        act(Sp[:, par, c + 1, :], S, AF.Identity, [SK, ("smH", h)], [("Sp", par, c + 1)],
                            scale=smH[:, h, 0, c + 1:c + 2])
                for j in range(NB):
                    js = slice(j * 128, (j + 1) * 128)
                    mm(obank[:, js], vtm[:, j, h * 128:(h + 1) * 128], AT[:, par, j, :], j == 0, False,
                       [("vtm", j), ("AT", par)], [obk])
                for c in range(NCH):
                    cs = slice(c * 64, (c + 1) * 64)
                    mm(obank[:, cs], Sp[:, par, c, :], qT[:, h, cs], False, c == NCH - 1,
                       [("Sp", par, c), ("qT", h)], [obk])
                act(pb[:, par, :], obank, AF.Square, [obk], [("pb", par)])
                mm(mb[0], onesB, pb[:, par, :], True, True, [("pb", par), "const"], ["mb0"])
                rs, rk = Ft[:, 3 + par, 0:T], ("Ft", 3 + par)
                act(rs, mb[0], AF.Ln, ["mb0"], [rk], bias=EPS, scale=1.0 / 128)
                act(rs, rs, AF.Exp, [rk], [rk], scale=-0.5)
                tt("pool", rs, rs, sg[:, h, :], ALU.mult, [rk, ("D4", h)], [rk])
                stt(mixed[:, h, :], obank, V(V_HGN + h), rs, ALU.mult, ALU.mult, [obk, rk, "vec"], [("mixed", h)])
                yield

        ppmod[0] = 2
        _interleave([hg_stream((0, 2, 4), [(mb[1], "mb1"), (mb[2], "mb2")]),
                     hg_stream((1, 3, 5), [(mb[3], "mb3"), (mb[4], "mb4")]),
                     stream_b()], 3)
        ppmod[0] = 3

        chk("E11")
        D4K = [("D4", h) for h in range(6)]
        E4K = [("ktm", 0), ("ktm", 1)]
        for j in range(NB):
            js = slice(j * 128, (j + 1) * 128)
            for b in range(6):
                tp(mbbf[4][:, b * 128:(b + 1) * 128], xs[:, b, js], identB, [("xs", b), "const"], ["mb4"])
            if True:
                cp("act", xstm[:, j, :], mbbf[4][:, 0:768], ["mb4"], [("xstm", j)])
            if True:
                tt("dve", xdtd[:, j, :].rearrange("p (h q) -> p h q", q=64),
                   xstm[:, j, :].rearrange("p (h q) -> p h q", q=64),
                   dtd[:, j * 12:(j + 1) * 12].unsqueeze(2).to_broadcast([128, 12, 64]), ALU.mult, [("xstm", j)] + TM, D4K)
            if True:
                for g in range(4):
                    tp(mbbf[4][:, g * 128:(g + 1) * 128], bc[:, g, js], identB, [("bc", g), "const"], ["mb4"])
            if True:
                cp("act", bmtm[:, j, :], mbbf[4][:, 0:512], ["mb4"], E4K)
        chk("S1")
        YB = [(mb[2], "mb2"), (mb[3], "mb3"), (mb[4], "mb4")]

        def ssd_cb(j):
            js = slice(j * 128, (j + 1) * 128)
            for g in range(4):
                mm(mb[0][:, g * 128:(g + 1) * 128], bc[:, g, js], bc[:, 4 + g, js], True, True,
                   [("bc", g), ("bc", 4 + g)], ["mb0"])

        def ssd_chunk(j, g, cc):
            ybank, ybk = YB[(j * 4 + g) % 3]
            HK = ("hst", L, g)
            rows = slice(cc * 64, (cc + 1) * 64)
            cols = slice(j * 128 + cc * 64, j * 128 + (cc + 1) * 64)
            mm(ybank[rows, 192:384], bc[:, 4 + g, cols], hbf[:, L, g, :], True, True,
               [("bc", 4 + g), ("hbf", L, g)], [ybk])
            stb, stk = next_pp()
            stp = stb[:, 0:192]
            mm(stp, bmtm[rows, j, g * 128:(g + 1) * 128], xdtd[rows, j, g * 192:(g + 1) * 192], True, True,
               E4K + D4K, [stk])
            hv = hst[:, L, g, :]
            tt("dve", hv.rearrange("p (h q) -> p h q", q=64), hv.rearrange("p (h q) -> p h q", q=64),
               dcyb[:, cc, j * 12 + 3 * g:j * 12 + 3 * g + 3].unsqueeze(2).to_broadcast([128, 3, 64]),
               ALU.mult, [HK, "dcyb"], [HK])
            tt("dve", hv, hv, stp, ALU.add, [stk, HK], [HK])
            cp("dve", hbf[:, L, g, :], hv, [HK], [("hbf", L, g)])

        def ssd_A(j, g):
            if g == 0:
                ssd_cb(j)
            ssd_chunk(j, g, 0)
            yield
            for hh in range(3):
                h = 3 * g + hh
                idx = j * 12 + h
                dps = mb[1][:, 0:128]
                dk = "mb1"
                mm(dps, dhl[:, 0, idx:idx + 1].to_broadcast([128, 128]), mask01, True, False, ["dhl", "const"], [dk])
                mm(dps, dhl[:, 1, idx:idx + 1].to_broadcast([128, 128]), mask01, False, False, ["dhl", "const"], [dk])
                mm(dps, identB, mneg, False, True, ["const"], [dk])
                si = idx % 8
                act(seg[:, si, :], dps, AF.Exp, [dk] + TM, [("seg", si)], bias=bias_tm[:, idx:idx + 1])
                yield
                tt("dve", LTb[:, si, :], mb[0][:, g * 128:(g + 1) * 128], seg[:, si, :], ALU.mult,
                   ["mb0", ("seg", si)], [("LT", si)])
                yield

        def ssd_B(j, g):
            ybank, ybk = YB[(j * 4 + g) % 3]
            for hh in range(3):
                h = 3 * g + hh
                si = (j * 12 + h) % 8
                mm(ybank[:, hh * 64:(hh + 1) * 64], LTb[:, si, :], xstm[:, j, h * 64:(h + 1) * 64], True, False,
                   [("LT", si), ("xstm", j)], [ybk])
                mm(ybank[:, hh * 64:(hh + 1) * 64], idsk[:, L, h, :], xstm[:, j, h * 64:(h + 1) * 64], False, True,
                   ["idsk", ("xstm", j)], [ybk])
                yield
            ssd_chunk(j, g, 1)
            yield

        def ssd_C(j, g):
            ybank, ybk = YB[(j * 4 + g) % 3]
            yi = (j * 4 + g) % 2
            tmp = ytmp[:, yi, :]
            idx0 = j * 12 + 3 * g
            tt("dve", tmp.rearrange("p (h q) -> p h q", q=64), ybank[:, 192:384].rearrange("p (h q) -> p h q", q=64),
               ecum[:, idx0:idx0 + 3].unsqueeze(2).to_broadcast([128, 3, 64]), ALU.mult, [ybk] + TM, [("ytmp", yi)])
            yield
            tt("dve", tmp, tmp, ybank[:, 0:192], ALU.add, [ybk, ("ytmp", yi)], [("ytmp", yi)])
            yield
            tt("dve", ysb[:, j % 2, g * 192:(g + 1) * 192], tmp, G4[:, j, g * 192:(g + 1) * 192], ALU.mult,
               [("ytmp", yi), ("G4", j)], [("ysb", j % 2, g)])
            yield

        def ssd_post(j):
            js = slice(j * 128, (j + 1) * 128)
            jp = j % 2
            YK = [("ysb", jp, g) for g in range(4)]
            yv, ov, sv = ysb[:, jp, :], otm[:, jp, :], ssg[:, jp, :]
            OK_, SK_ = ("otm", jp), ("ssg", jp)
            for g in range(4):
                act(ov[:, g * 192:(g + 1) * 192], yv[:, g * 192:(g + 1) * 192], AF.Square, YK, [OK_, SK_],
                    accum_out=sv[:, g:g + 1])
                yield
            act(sv[:, 4:8], sv[:, 0:4], AF.Ln, [SK_], [SK_], bias=EPS, scale=1.0 / 192)
            act(sv[:, 4:8], sv[:, 4:8], AF.Exp, [SK_], [SK_], scale=-0.5)
            yield
            tt("dve", ov.rearrange("p (g q) -> p g q", q=192), yv.rearrange("p (g q) -> p g q", q=192),
               sv[:, 4:8].unsqueeze(2).to_broadcast([128, 4, 192]), ALU.mult, YK + [SK_], [OK_])
            yield
            tb, tk = next_pp()
            tbb = tb.bitcast(BF16)
            for b in range(6):
                tp(tbb[:, b * 128:(b + 1) * 128], ov[:, b * 128:(b + 1) * 128], identB, [OK_, "const"], [tk])
            yield
            for b in range(6):
                act(mixed[:, 10 + b, js], tbb[:, b * 128:(b + 1) * 128], AF.Identity, [tk, "vec"], [("mixed", 10 + b)],
                    scale=V(V_SSDN + b))
                yield

        its = [(j, g) for j in range(NB) for g in range(4)]
        n_it = len(its)
        for k in range(n_it + 3):
            tasks = []
            if k < n_it:
                tasks.append(ssd_A(*its[k]))
            if 1 <= k <= n_it:
                tasks.append(ssd_B(*its[k - 1]))
            if 2 <= k <= n_it + 1:
                tasks.append(ssd_C(*its[k - 2]))
            if k >= 3 and its[k - 3][1] == 3:
                tasks.append(ssd_post(its[k - 3][0]))
            _interleave(tasks, 4)

        chk("E12")
        for og in range(4):
            slot, wk = use_w(f"WO{og}")
            wv = slot[:, 0:4096].rearrange("p (k n) -> p k n", k=16)
            for oo in range(2):
                ob = og * 2 + oo
                bank, bk = next_pp()
                for kc in range(16):
                    mm(bank, wv[:, kc, oo * 128:(oo + 1) * 128], mixed[:, kc, :], kc == 0, kc == 15,
                       [wk, ("mixed", kc)], [bk])
                tt("dve", hT[:, ob, :], bank, hT[:, ob, :], ALU.add, [bk, ("hT", ob)], [("hT", ob)])
                cp("act", uT[:, ob, :], hT[:, ob, :], [("hT", ob)], [("uT", ob)])
        slotG, wkG = use_w("WG0")
        slotP, wkP = use_w("WPE")
        wpe = slotP[:, 0:2048].rearrange("p (j n) -> p j n", j=2)
        for half in range(2):
            if half == 1:
                slotG, wkG = use_w("WG1")
            wvg = wview(slotG, 512)
            for oo in range(4):
                ob = half * 4 + oo
                bankg, bkg = next_pp()
                for c in range(8):
                    mm(bankg, wvg[:, c, oo * 128:(oo + 1) * 128], uT[:, c, :], c == 0, c == 7, [wkG, ("uT", c)], [bkg])
                bankp, bkp = next_pp()
                for jj in range(2):
                    mm(bankp, wpe[:, jj, ob * 128:(ob + 1) * 128], pT[:, jj, :], jj == 0, jj == 1, [wkP, ("pT", jj)], [bkp])
                r = ob % 3
                tg, tk = Ft[:, r, 0:T], ("Ft", r)
                act(tg, bankg, AF.Tanh, [bkg], [tk], scale=0.5)
                stt(tg, tg, 1.0, bankp, ALU.add, ALU.mult, [tk, bkp], [tk])
                stt(hT[:, ob, :], tg, 0.5, hT[:, ob, :], ALU.mult, ALU.add, [tk, ("hT", ob)], [("hT", ob)])

        if dbg is not None and dbg == (it, L):
            dma("sp", dbg_d, hT, [("hT", c) for c in range(8)], ["dbgd"], "dbg")

        if last_layer:
            rms_stats(8, 1.0 / D)
            for c in range(8):
                stt(hT[:, c, :], hT[:, c, :], V(V_FNW + c), Ft[:, 5, 0:T], ALU.mult, ALU.mult,
                    [("hT", c), ("Ft", 5), "vec"], [("hT", c)])
            for j in range(NB):
                js = slice(j * 128, (j + 1) * 128)
                for cq in range(2):
                    bank, bk = next_pp()
                    for c4 in range(4):
                        c = cq * 4 + c4
                        tp(bank[:, c4 * 128:(c4 + 1) * 128], hT[:, c, js], identF, [("hT", c), "const"], [bk])
                    cp("act" if (j + cq) % 2 == 0 else "dve", iof[:, j, cq * 512:(cq + 1) * 512], bank, [bk], MIXK)
            dma("sp", out_d[t0:t0 + T, :].rearrange("(j p) d -> p j d", p=128), iof, MIXK, ["outd"], "out")

    try:
        for it in range(n_tiles):
            for L in range(n_layers):
                tile_layer(it, L, L == n_layers - 1)
    except _Stop:
        pass

    finals = [s for s in ("out", "dbg") if s in pr.dma_count]
    pr.emit(final_waits=finals)
    return nc


_CACHE = {}


def kernel(x, p, norm_w, w_in, hg_lb, hg_norm_w, pool_w, pool_scale, conv_w, conv_b, dt_bias, a_log, d_skip,
           ssd_norm_w, w_out, w_pe, w_pg, final_norm_w):
    f = lambda a: np.ascontiguousarray(np.asarray(a, dtype=np.float32))
    x, p = f(x), f(p)
    wcat = _prep_weights(f(w_in), f(w_out), f(w_pe), f(w_pg))
    vecs = _prep_vecs(f(norm_w), f(hg_lb), f(hg_norm_w), f(pool_scale), f(conv_w), f(conv_b), f(dt_bias), f(a_log),
                      f(d_skip), f(ssd_norm_w), f(final_norm_w))
    poolw = np.ascontiguousarray(f(pool_w).transpose(0, 2, 1, 3).reshape(2, 128, 512))
    if "nc" not in _CACHE:
        _CACHE["nc"] = build_program()
    nc = _CACHE["nc"]
    B = x.shape[0]
    in_maps = [{"x": x[b], "p": np.ascontiguousarray(p[:, b]), "wcat": wcat, "vecs": vecs, "poolw": poolw}
               for b in range(B)]
    res = run_bass_kernel_spmd(nc, in_maps, core_ids=list(range(B)))
    return np.stack([r["out"] for r in res.results], axis=0).astype(np.float32)
```
